# Optimizing a Trainium2 kernel written in Bass

```python
import math
import jax, jax.numpy as jnp
from jax import lax
import numpy as np

D_MODEL = 2048
BATCH = 1
SEQ = 8192
DEPTH = 2

GRID_W = 64
CTX_LEN = 256
N_BRANCH = 4
BRANCH_W = D_MODEL // N_BRANCH
CONV_CH = BRANCH_W
CONV_WIDTH = 31
GQA_HEAD_DIM = 128
GQA_HEADS = BRANCH_W // GQA_HEAD_DIM
GQA_KV_HEADS = GQA_HEADS // 2
MLA_NOPE = 128
MLA_ROPE = 64
MLA_V_DIM = 128
MLA_HEADS = BRANCH_W // MLA_V_DIM
MLA_Q_RANK = BRANCH_W
MLA_KV_RANK = BRANCH_W // 2
DIFF_QK = 64
DIFF_HEADS = BRANCH_W // (2 * DIFF_QK)
D_FF = ((8 * D_MODEL // 3 + 255) // 256) * 256
FFN_CONV_WIDTH = 3
Q_BLOCK = 128
ROPE_THETA = 10000.0
EPS = 1e-6

KV_SIZES = (GQA_KV_HEADS * GQA_HEAD_DIM, GQA_KV_HEADS * GQA_HEAD_DIM,
            MLA_KV_RANK, MLA_ROPE,
            DIFF_HEADS * 2 * DIFF_QK, DIFF_HEADS * 2 * DIFF_QK)
Q_SIZES = (2 * CONV_CH, GQA_HEADS * GQA_HEAD_DIM, MLA_Q_RANK,
           DIFF_HEADS * 2 * DIFF_QK, N_BRANCH * D_MODEL)
N_KV = sum(KV_SIZES)
N_IN = N_KV + sum(Q_SIZES)

kernel_name = "hybrid_parallel_mixer_dit_block"


def _split(z, sizes):
    offs = np.cumsum(sizes)[:-1].tolist()
    return jnp.split(z, offs, axis=-1)


def _rms_norm(x, g):
    xf = x.astype(jnp.float32)
    y = xf * lax.rsqrt(jnp.mean(xf * xf, axis=-1, keepdims=True) + EPS)
    return (y * g.astype(jnp.float32)).astype(x.dtype)


def _layer_norm(x, g, b):
    xf = x.astype(jnp.float32)
    mu = jnp.mean(xf, axis=-1, keepdims=True)
    var = jnp.mean(jnp.square(xf - mu), axis=-1, keepdims=True)
    y = (xf - mu) * lax.rsqrt(var + EPS)
    return (y * g.astype(jnp.float32) + b.astype(jnp.float32)).astype(x.dtype)


def _modulate(h, shift, scale):
    return h * (1.0 + scale) + shift


def _heads(t, n):
    return t.reshape(t.shape[0], t.shape[1], n, -1)


def _rope_tables(row, col, rot_dim):
    axis_dim = rot_dim // 2
    inv = jnp.power(ROPE_THETA, -jnp.arange(0, axis_dim, 2, dtype=jnp.float32) / axis_dim)
    ang = jnp.concatenate([row.astype(jnp.float32)[:, None] * inv,
                           col.astype(jnp.float32)[:, None] * inv], axis=-1)
    return jnp.cos(ang), jnp.sin(ang)


def _apply_rope(x, cs):
    cos, sin = cs
    cos = cos[None, :, None, :].astype(x.dtype)
    sin = sin[None, :, None, :].astype(x.dtype)
    x1, x2 = jnp.split(x, 2, axis=-1)
    return jnp.concatenate([x1 * cos - x2 * sin, x2 * cos + x1 * sin], axis=-1)


def _dwconv(u, w, b):
    pad = w.shape[0] // 2
    y = lax.conv_general_dilated(u, w[:, None, :].astype(u.dtype), window_strides=(1,),
                                 padding=[(pad, pad)], dimension_numbers=('NWC', 'WIO', 'NWC'),
                                 feature_group_count=u.shape[-1])
    return y + b.astype(u.dtype)


def _attend(q, k, v, scale):
    B, L, H, dk = q.shape
    Hk = k.shape[2]
    G = H // Hk
    nb = L // Q_BLOCK
    qb = q.reshape(B, nb, Q_BLOCK, Hk, G, dk).swapaxes(0, 1)

    def body(qi):
        s = jnp.einsum('bqkgd,bskd->bkgqs', qi, k).astype(jnp.float32) * scale
        pr = jax.nn.softmax(s, axis=-1).astype(v.dtype)
        return jnp.einsum('bkgqs,bskd->bqkgd', pr, v)

    o = lax.map(body, qb)
    return o.swapaxes(0, 1).reshape(B, L, H * v.shape[-1])


def _diff_attend(q1, q2, k1, k2, v, lam, scale):
    B, L, H, d = q1.shape
    nb = L // Q_BLOCK

    def blocks(t):
        return t.reshape(B, nb, Q_BLOCK, H, d).swapaxes(0, 1)

    def body(qs):
        qa, qb = qs
        s1 = jnp.einsum('bqhd,bshd->bhqs', qa, k1).astype(jnp.float32) * scale
        s2 = jnp.einsum('bqhd,bshd->bhqs', qb, k2).astype(jnp.float32) * scale
        pr = jax.nn.softmax(s1, axis=-1) - lam * jax.nn.softmax(s2, axis=-1)
        return jnp.einsum('bhqs,bshd->bqhd', pr.astype(v.dtype), v)

    o = lax.map(body, (blocks(q1), blocks(q2)))
    return o.swapaxes(0, 1).reshape(B, L, H, v.shape[-1])


def _kv_side(zkv, p, rope):
    B, L = zkv.shape[:2]
    k_g, v_g, kv_lat, k_r, k_d, v_d = _split(zkv, KV_SIZES)
    k_g = _rms_norm(_heads(k_g, GQA_KV_HEADS), p['gqa_k_g'])
    v_g = _heads(v_g, GQA_KV_HEADS)
    kv = _heads(_rms_norm(kv_lat, p['mla_kv_g']) @ p['mla_w_kv_up'], MLA_HEADS)
    k_nope, v_m = kv[..., :MLA_NOPE], kv[..., MLA_NOPE:]
    k_r = k_r[:, :, None, :]
    k_d = k_d.reshape(B, L, DIFF_HEADS, 2, DIFF_QK)
    k1, k2 = k_d[..., 0, :], k_d[..., 1, :]
    v_d = _heads(v_d, DIFF_HEADS)
    if rope is not None:
        k_g = _apply_rope(k_g, rope[0])
        k_r = _apply_rope(k_r, rope[1])
        k1 = _apply_rope(k1, rope[2])
        k2 = _apply_rope(k2, rope[2])
    k_m = jnp.concatenate([k_nope, jnp.broadcast_to(k_r, (B, L, MLA_HEADS, MLA_ROPE))], axis=-1)
    return (k_g, v_g, k_m, v_m, k1, k2, v_d)


def _conformer_conv(u, p):
    a, g = jnp.split(u, 2, axis=-1)
    u = _dwconv(a * jax.nn.sigmoid(g), p['conv_w'], p['conv_b'])
    return jax.nn.silu(_layer_norm(u, p['conv_ln_g'], p['conv_ln_b']))


def _mixer_out(zq, kv, p, lam_init, rope):
    B, L = zq.shape[:2]
    glu, q_g, q_lat, q_d, gate_logits = _split(zq, Q_SIZES)
    k_g, v_g, k_m, v_m, k1, k2, v_d = kv
    y_a = _conformer_conv(glu, p)
    qg = _rms_norm(_heads(q_g, GQA_HEADS), p['gqa_q_g'])
    if rope is not None:
        qg = _apply_rope(qg, rope[0])
    y_b = _attend(qg, k_g, v_g, GQA_HEAD_DIM ** -0.5)
    qm = _heads(_rms_norm(q_lat, p['mla_q_g']) @ p['mla_w_q_up'], MLA_HEADS)
    q_nope, q_rope = qm[..., :MLA_NOPE], qm[..., MLA_NOPE:]
    if rope is not None:
        q_rope = _apply_rope(q_rope, rope[1])
    y_c = _attend(jnp.concatenate([q_nope, q_rope], axis=-1), k_m, v_m, (MLA_NOPE + MLA_ROPE) ** -0.5)
    qd = q_d.reshape(B, L, DIFF_HEADS, 2, DIFF_QK)
    q1, q2 = qd[..., 0, :], qd[..., 1, :]
    if rope is not None:
        q1 = _apply_rope(q1, rope[2])
        q2 = _apply_rope(q2, rope[2])
    f32 = jnp.float32
    lam = (jnp.exp(jnp.sum(p['diff_lq1'].astype(f32) * p['diff_lk1'].astype(f32)))
           - jnp.exp(jnp.sum(p['diff_lq2'].astype(f32) * p['diff_lk2'].astype(f32))) + lam_init)
    od = _diff_attend(q1, q2, k1, k2, v_d, lam, DIFF_QK ** -0.5)
    y_d = (_rms_norm(od, p['diff_g']) * (1.0 - lam_init)).reshape(B, L, -1)
    br = jnp.stack([y_a, y_b, y_c, y_d], axis=2)
    proj = jnp.einsum('blnc,ncd->blnd', br, p['w_branch'])
    gates = jax.nn.sigmoid(gate_logits.reshape(B, L, N_BRANCH, D_MODEL))
    return jnp.einsum('blnd,blnd->bld', gates, proj) @ p['w_out']


def _mixer_sublayer(h, hc, p, lam_init, rope, last):
    z = h @ p['w_in']
    kv_lat = _kv_side(z[..., :N_KV], p, rope)
    w_c = p['w_in'][:, :N_KV] if last else p['w_in']
    zc = hc @ w_c
    kv_ctx = _kv_side(zc[..., :N_KV], p, None)
    kv_all = tuple(jnp.concatenate([a, b], axis=1) for a, b in zip(kv_ctx, kv_lat))
    y = _mixer_out(z[..., N_KV:], kv_all, p, lam_init, rope)
    yc = None if last else _mixer_out(zc[..., N_KV:], kv_ctx, p, lam_init, None)
    return y, yc


def _conv_ffn(h, p):
    u = _dwconv(h @ p['ffn_w_up'], p['ffn_dw_w'], p['ffn_dw_b'])
    a, b = jnp.split(u, 2, axis=-1)
    return (jax.nn.silu(a) * b) @ p['ffn_w_down']


def setup_inputs(seed: int = 0) -> dict:
    key = jax.random.key(seed)
    ks = jax.random.split(key, 32)

    def nrm(k, shape, scale):
        return jax.random.normal(k, shape, jnp.float32) * scale

    def gain(k, shape):
        return 1.0 + nrm(k, shape, 0.01)

    L_ = DEPTH
    return {
        "x": nrm(ks[0], (BATCH, SEQ, D_MODEL), 1.0),
        "c": nrm(ks[1], (BATCH, D_MODEL), 1.0),
        "ctx": nrm(ks[2], (BATCH, CTX_LEN, D_MODEL), 1.0),
        "c_ctx": nrm(ks[3], (D_MODEL,), 1.0),
        "w_mod": nrm(ks[4], (L_, D_MODEL, 6 * D_MODEL), 0.5 * D_MODEL ** -0.5),
        "b_mod": nrm(ks[5], (L_, 6 * D_MODEL), 0.01),
        "norm1_g": gain(ks[6], (L_, D_MODEL)),
        "norm2_g": gain(ks[7], (L_, D_MODEL)),
        "w_in": nrm(ks[8], (L_, D_MODEL, N_IN), D_MODEL ** -0.5),
        "conv_w": nrm(ks[9], (L_, CONV_WIDTH, CONV_CH), CONV_WIDTH ** -0.5),
        "conv_b": nrm(ks[10], (L_, CONV_CH), 0.01),
        "conv_ln_g": gain(ks[11], (L_, CONV_CH)),
        "conv_ln_b": nrm(ks[12], (L_, CONV_CH), 0.01),
        "gqa_q_g": gain(ks[13], (L_, GQA_HEAD_DIM)),
        "gqa_k_g": gain(ks[14], (L_, GQA_HEAD_DIM)),
        "mla_q_g": gain(ks[15], (L_, MLA_Q_RANK)),
        "mla_w_q_up": nrm(ks[16], (L_, MLA_Q_RANK, MLA_HEADS * (MLA_NOPE + MLA_ROPE)), MLA_Q_RANK ** -0.5),
        "mla_kv_g": gain(ks[17], (L_, MLA_KV_RANK)),
        "mla_w_kv_up": nrm(ks[18], (L_, MLA_KV_RANK, MLA_HEADS * (MLA_NOPE + MLA_V_DIM)), MLA_KV_RANK ** -0.5),
        "diff_lq1": nrm(ks[19], (L_, DIFF_QK), 0.1),
        "diff_lk1": nrm(ks[20], (L_, DIFF_QK), 0.1),
        "diff_lq2": nrm(ks[21], (L_, DIFF_QK), 0.1),
        "diff_lk2": nrm(ks[22], (L_, DIFF_QK), 0.1),
        "diff_g": gain(ks[23], (L_, 2 * DIFF_QK)),
        "w_branch": nrm(ks[24], (L_, N_BRANCH, BRANCH_W, D_MODEL), BRANCH_W ** -0.5),
        "w_out": nrm(ks[25], (L_, D_MODEL, D_MODEL), D_MODEL ** -0.5),
        "ffn_w_up": nrm(ks[26], (L_, D_MODEL, 2 * D_FF), D_MODEL ** -0.5),
        "ffn_dw_w": nrm(ks[27], (L_, FFN_CONV_WIDTH, 2 * D_FF), FFN_CONV_WIDTH ** -0.5),
        "ffn_dw_b": nrm(ks[28], (L_, 2 * D_FF), 0.01),
        "ffn_w_down": nrm(ks[29], (L_, D_FF, D_MODEL), D_FF ** -0.5),
        "final_g": gain(ks[30], (D_MODEL,)),
    }


def reference(x, c, ctx, c_ctx, w_mod, b_mod, norm1_g, norm2_g, w_in, conv_w, conv_b,
              conv_ln_g, conv_ln_b, gqa_q_g, gqa_k_g, mla_q_g, mla_w_q_up, mla_kv_g,
              mla_w_kv_up, diff_lq1, diff_lk1, diff_lq2, diff_lk2, diff_g, w_branch, w_out,
              ffn_w_up, ffn_dw_w, ffn_dw_b, ffn_w_down, final_g):
    rows = x.shape[1] // GRID_W
    t = jnp.arange(rows * GRID_W)
    row, col = t // GRID_W, t % GRID_W
    rope = (_rope_tables(row, col, GQA_HEAD_DIM),
            _rope_tables(row, col, MLA_ROPE),
            _rope_tables(row, col, DIFF_QK))
    xc = ctx
    for l in range(DEPTH):
        last = l == DEPTH - 1
        p = {
            'w_in': w_in[l], 'conv_w': conv_w[l], 'conv_b': conv_b[l],
            'conv_ln_g': conv_ln_g[l], 'conv_ln_b': conv_ln_b[l],
            'gqa_q_g': gqa_q_g[l], 'gqa_k_g': gqa_k_g[l],
            'mla_q_g': mla_q_g[l], 'mla_w_q_up': mla_w_q_up[l],
            'mla_kv_g': mla_kv_g[l], 'mla_w_kv_up': mla_w_kv_up[l],
            'diff_lq1': diff_lq1[l], 'diff_lk1': diff_lk1[l],
            'diff_lq2': diff_lq2[l], 'diff_lk2': diff_lk2[l], 'diff_g': diff_g[l],
            'w_branch': w_branch[l], 'w_out': w_out[l],
            'ffn_w_up': ffn_w_up[l], 'ffn_dw_w': ffn_dw_w[l], 'ffn_dw_b': ffn_dw_b[l],
            'ffn_w_down': ffn_w_down[l],
        }
        mod = jax.nn.silu(c) @ w_mod[l] + b_mod[l]
        sh1, sc1, g1, sh2, sc2, g2 = [m[:, None, :] for m in jnp.split(mod, 6, axis=-1)]
        modc = jax.nn.silu(c_ctx) @ w_mod[l] + b_mod[l]
        csh1, csc1, cg1, csh2, csc2, cg2 = jnp.split(modc, 6, axis=-1)
        lam_init = 0.8 - 0.6 * math.exp(-0.3 * l)
        h = _modulate(_rms_norm(x, norm1_g[l]), sh1, sc1)
        hc = _modulate(_rms_norm(xc, norm1_g[l]), csh1, csc1)
        y, yc = _mixer_sublayer(h, hc, p, lam_init, rope, last)
        x = x + g1 * y
        h = _modulate(_rms_norm(x, norm2_g[l]), sh2, sc2)
        x = x + g2 * _conv_ffn(h, p)
        if not last:
            xc = xc + cg1 * yc
            hc = _modulate(_rms_norm(xc, norm2_g[l]), csh2, csc2)
            xc = xc + cg2 * _conv_ffn(hc, p)
    return _rms_norm(x, final_g)
```

```python
import math
import numpy as np
import ml_dtypes
import concourse.bass as bass
import concourse.mybir as mybir
from concourse.bass_utils import run_bass_kernel_spmd
from contextlib import ExitStack

F32 = mybir.dt.float32
BF16 = mybir.dt.bfloat16
AF = mybir.ActivationFunctionType
ALU = mybir.AluOpType
AX = mybir.AxisListType

D = 2048
SEQ = 8192
NCORE = 8
TL = 1024
TC = 256
T = TL + TC
DEPTH = 2
N_KV = 1856
N_IN = 12608
DFF = 5632
EPS = 1e-6
KROWS = 1344
VCOLS = 1280
QROWS = 1792
GROUPS = [(0, 512), (512, 512), (1024, 256)]
MODC = 6 * D // NCORE
BIGW = [("w_in", D, N_IN), ("w_branch", D, D), ("w_out", D, D), ("ffn_w_up", D, 2 * DFF), ("ffn_w_down", DFF, D)]


def V(name, *a, **kw):
    return lambda e: getattr(e, name)(*a, **kw)


class _Op:
    __slots__ = ("eng", "fn", "deps", "signals", "dma_key", "clk", "val", "vc", "waits", "inc", "ep", "det")

    def __init__(self, eng, fn, dma_key, inc):
        self.eng = eng
        self.fn = fn
        self.deps = []
        self.signals = False
        self.dma_key = dma_key
        self.clk = None
        self.val = 0
        self.vc = None
        self.waits = []
        self.inc = inc


class Sched:
    def __init__(self, nc, sync_same=True):
        self.nc = nc
        self.ops = []
        self.last_writer = {}
        self.readers = {}
        self.sync_same = sync_same
        self.epoch = 0

    def op(self, eng, fn, reads=(), writes=(), dma_key=None, inc=16, detached=False):
        o = _Op(eng, fn, dma_key, inc)
        o.ep = self.epoch
        o.det = detached
        idx = len(self.ops)
        deps = set()
        for k in reads:
            w = self.last_writer.get(k)
            if w is not None:
                deps.add(w)
        for k in writes:
            w = self.last_writer.get(k)
            if w is not None:
                deps.add(w)
            for r in self.readers.get(k, ()):
                deps.add(r)
        for d in deps:
            p = self.ops[d]
            if p.dma_key is None and p.eng == eng:
                if eng == "pe" or eng == "sp":
                    continue
                if not self.sync_same:
                    continue
                raw = any(self.last_writer.get(k) == d for k in reads) or any(
                    self.last_writer.get(k) == d for k in writes)
                if not raw:
                    continue
            o.deps.append(d)
            p.signals = True
        for k in reads:
            self.readers.setdefault(k, []).append(idx)
        for k in writes:
            self.last_writer[k] = idx
            self.readers[k] = []
        if dma_key is not None:
            o.signals = True
        self.ops.append(o)
        return idx

    def barrier(self):
        seen = set()
        for p in reversed(self.ops):
            if p.eng == "*barrier*":
                break
            if p.eng not in seen:
                seen.add(p.eng)
                p.signals = True
        bo = _Op("*barrier*", None, None, 0)
        bo.det = False
        self.ops.append(bo)
        self.last_writer = {k: w for k, w in self.last_writer.items() if self.ops[w].det}
        self.readers = {}

    def finalize_and_emit(self):
        nc = self.nc
        counts = {}
        known = {}
        ENGS = ("pe", "act", "dve", "pool", "sp")
        det_clks = set(("dma", o.dma_key) for o in self.ops if o.eng != "*barrier*" and o.det)
        for o in self.ops:
            E = o.eng
            if E == "*barrier*":
                bw = {}
                for EE in ENGS:
                    kn = known.setdefault(EE, {})
                    bw[EE] = [(c, v) for c, v in counts.items() if kn.get(c, 0) < v and c not in det_clks]
                    for c, v in counts.items():
                        if c not in det_clks:
                            kn[c] = max(kn.get(c, 0), v)
                o.waits = bw
                continue
            kn = known.setdefault(E, {})
            for d in sorted(o.deps):
                p = self.ops[d]
                if kn.get(p.clk, 0) < p.val:
                    o.waits.append((p.clk, p.val))
                    for c, v in p.vc.items():
                        if kn.get(c, 0) < v:
                            kn[c] = v
                    kn[p.clk] = p.val
            best = {}
            for c, v in o.waits:
                best[c] = max(best.get(c, 0), v)
            o.waits = list(best.items())
            if o.signals:
                if o.dma_key is not None:
                    o.clk = ("dma", o.dma_key)
                    counts[o.clk] = counts.get(o.clk, 0) + o.inc
                else:
                    o.clk = (E, o.ep)
                    counts[o.clk] = counts.get(o.clk, 0) + 1
                o.val = counts[o.clk]
                o.vc = dict(kn)
        clks = list(counts.keys())
        self.n_sems = len(clks)
        self.max_count = max(counts.values()) if counts else 0
        with ExitStack() as es:
            sems = {}
            for i, c in enumerate(clks):
                sems[c] = es.enter_context(nc.semaphore("s%d" % i))
            block = es.enter_context(nc.Block())
            per_eng = {}
            for o in self.ops:
                if o.eng == "*barrier*":
                    for EE in ENGS:
                        per_eng.setdefault(EE, []).append(o)
                else:
                    per_eng.setdefault(o.eng, []).append(o)
            final = dict(counts)

            def emit(engh, lst, E):
                for o in lst:
                    if o.eng == "*barrier*":
                        for c, v in o.waits[E]:
                            engh.wait_ge(sems[c], v)
                        continue
                    for c, v in o.waits:
                        engh.wait_ge(sems[c], v)
                    ins = o.fn(engh)
                    if o.signals:
                        ins.then_inc(sems[o.clk], o.inc if o.dma_key is not None else 1)
                if E == "sp":
                    for c, v in final.items():
                        engh.wait_ge(sems[c], v)

            names = {"pe": "tensor", "act": "scalar", "dve": "vector", "pool": "gpsimd", "sp": "sync"}
            for E in ENGS:
                lst = per_eng.get(E, [])

                def mk(lst=lst, E=E):
                    def f(engh):
                        emit(engh, lst, E)
                    return f
                getattr(block, names[E])(mk())


class Arena:
    def __init__(self, nc, es, nbytes):
        self.t = es.enter_context(nc.sbuf_tensor("arena", [128, nbytes // 2], BF16))
        self.nbytes = nbytes
        self.off = 0
        self.base = 0
        self.n = 0

    def reset(self):
        self.off = self.base

    def persist(self):
        self.base = self.off

    def tile(self, n, dt, parts=128):
        sz = 4 if dt == F32 else 2
        nb = (n * sz + 63) // 64 * 64
        assert self.off + nb <= self.nbytes, ("arena overflow", self.off, nb, self.nbytes)
        a = self.t[0:parts, self.off // 2:(self.off + nb) // 2]
        self.off += nb
        if dt == F32:
            a = a.bitcast(F32)
        self.n += 1
        return a[:, 0:n]


class Builder:
    def __init__(self, debug=(), stop_after=None):
        self.debug = set(debug)
        self.stop_after = stop_after
        self.nc = bass.Bass("TRN2", target_bir_lowering=False)
        self.uid = 0

    def key(self, base):
        self.uid += 1
        return "%s#%d" % (base, self.uid)

    def din(self, name, shape, dt=F32):
        return self.nc.dram_tensor(name, list(shape), dt, kind="ExternalInput").ap()

    def dscr(self, name, shape, dt):
        kind = "ExternalOutput" if name in self.debug else "Internal"
        return self.nc.dram_tensor(name, list(shape), dt, kind=kind).ap()

    def dma(self, out, in_, reads, writes, key, eng="sp", **kw):
        self.S.op(eng, V("dma_start", out=out, in_=in_, **kw), reads=reads, writes=writes, dma_key=key)

    def bank(self):
        b = self.bank_i
        self.bank_i = (self.bank_i + 1) % 8
        return b

    def mm(self, b, out, lhsT, rhs, start, stop, reads):
        self.S.op("pe", V("matmul", out, lhsT=lhsT, rhs=rhs, start=start, stop=stop),
                  reads=reads, writes=["ps%d" % b])

    def act(self, out, in_, func, reads, writes, **kw):
        self.S.op("act", V("activation", out=out, in_=in_, func=func, **kw), reads=reads, writes=writes)

    def dve(self, fn, reads, writes, eng="dve"):
        self.S.op(eng, fn, reads=reads, writes=writes)

    def ps(self, b, n, parts=128, dt=F32):
        p = self.psum[b]
        if dt == BF16:
            return p[0:parts, :].bitcast(BF16)[:, 0:n]
        return p[0:parts, 0:n]

    def wload(self, dst3, src2, key, nk, split=4, cast=False, rkey=None):
        step = max(1, nk // split)
        for k0 in range(0, nk, step):
            k1 = min(nk, k0 + step)
            src = src2[k0 * 128:k1 * 128, :].rearrange("(k p) n -> p k n", p=128)
            self.dma(dst3[:, k0:k1, :], src, [rkey] if rkey else [], [key], key, eng="pool" if cast else "sp")

    def load_T(self, dst, src_rows, n, rkey, wkey):
        tmp = self.tmpT
        k = self.key("tmpT")
        self.dma(tmp[0:n, :], src_rows, [], ["tmpT"], "tmpT")
        b = self.bank()
        self.S.op("pe", V("transpose", self.ps(b, n), tmp[0:n, :], self.ident_f[0:n, 0:n]),
                  reads=["tmpT", "const"], writes=["ps%d" % b])
        self.act(dst, self.ps(b, n), AF.Copy, [], ["ps%d" % b, wkey])

    def build(self):
        nc = self.nc
        B = self
        I = {}
        I["x"] = self.din("x", [TL, D])
        I["ctx"] = self.din("ctx", [TC, D])
        I["cc"] = self.din("cc", [2, D])
        for nm, shp in [("w_mod", [DEPTH, D, MODC]), ("b_mod", [DEPTH, MODC]), ("norm1_g", [DEPTH, D]),
                        ("norm2_g", [DEPTH, D]), ("w_in", [DEPTH, D // NCORE, N_IN]), ("conv_w", [DEPTH, 31, 512]),
                        ("conv_b", [DEPTH, 512]), ("conv_ln_g", [DEPTH, 512]), ("conv_ln_b", [DEPTH, 512]),
                        ("gqa_q_g", [DEPTH, 128]), ("gqa_k_g", [DEPTH, 128]), ("mla_q_g", [DEPTH, 512]),
                        ("mla_w_q_up", [DEPTH, 512, 768]), ("mla_kv_g", [DEPTH, 256]),
                        ("mla_w_kv_up", [DEPTH, 256, 1024]), ("diff_lq1", [DEPTH, 64]), ("diff_lk1", [DEPTH, 64]),
                        ("diff_lq2", [DEPTH, 64]), ("diff_lk2", [DEPTH, 64]), ("diff_g", [DEPTH, 128]),
                        ("w_branch", [DEPTH, D // NCORE, D]), ("w_out", [DEPTH, D // NCORE, D]),
                        ("ffn_w_up", [DEPTH, D // NCORE, 2 * DFF]),
                        ("ffn_dw_w", [DEPTH, 3, 2 * DFF]), ("ffn_dw_b", [DEPTH, 2 * DFF]),
                        ("ffn_w_down", [DEPTH, DFF // NCORE, D]), ("final_g", [D])]:
            I[nm] = self.din(nm, shp)
        I["ropeg_c"] = self.din("ropeg_c", [128, T])
        I["ropeg_s"] = self.din("ropeg_s", [128, T])
        I["rope6_c"] = self.din("rope6_c", [128, T])
        I["rope6_s"] = self.din("rope6_s", [128, T])
        I["sel"] = self.din("sel", [128, 16])
        I["ident_f"] = self.din("ident_f", [128, 128])
        I["perm64"] = self.din("perm64", [128, 128])
        I["perm32"] = self.din("perm32", [128, 128])
        self.I = I
        out = nc.dram_tensor("out", [TL, D], F32, kind="ExternalOutput").ap()

        Sc = {}
        for nm, R, C in BIGW:
            Sc["sh_" + nm] = self.dscr("sh_" + nm, [DEPTH, R // NCORE, C], BF16)
            Sc["full_" + nm] = self.dscr("full_" + nm, [DEPTH, R, C], BF16)
        Sc["modp"] = self.dscr("modp", [DEPTH * 2, MODC], F32)
        Sc["modg"] = self.dscr("modg", [NCORE * DEPTH * 2, MODC], F32)
        Sc["xres"] = self.dscr("xres", [T, D], F32)
        Sc["modv"] = self.dscr("modv", [DEPTH, 2, 6 * D], F32)
        Sc["hT"] = self.dscr("hT", [16, 128, T], BF16)
        Sc["pay_k"] = self.dscr("pay_k", [KROWS, TL], BF16)
        Sc["kc"] = self.dscr("kc", [KROWS, TC], BF16)
        Sc["pay_v"] = self.dscr("pay_v", [TL, VCOLS], BF16)
        Sc["vc"] = self.dscr("vc", [TC, VCOLS], BF16)
        Sc["gk"] = self.dscr("gk", [NCORE * KROWS, TL], BF16)
        Sc["gv"] = self.dscr("gv", [NCORE * TL, VCOLS], BF16)
        Sc["sT"] = self.dscr("sT", [512, T], F32)
        Sc["pay_h"] = self.dscr("pay_h", [512, 30], F32)
        Sc["gh"] = self.dscr("gh", [NCORE * 512, 30], F32)
        Sc["qT"] = self.dscr("qT", [QROWS, T], BF16)
        Sc["brT"] = self.dscr("brT", [D, T], BF16)
        Sc["mT"] = self.dscr("mT", [D, T], BF16)
        Sc["pay_h2"] = self.dscr("pay_h2", [D, 2], BF16)
        Sc["gh2"] = self.dscr("gh2", [NCORE * D, 2], BF16)
        Sc["actT"] = self.dscr("actT", [DFF, T], BF16)
        self.Sc = Sc

        with ExitStack() as es:
            es.enter_context(nc.allow_non_contiguous_dma(reason="small strided vectors / boundary columns"))
            self.A = A = Arena(nc, es, 207 * 1024)
            self.psum = [es.enter_context(nc.psum_tensor("psb%d" % i, [128, 512], F32)) for i in range(8)]
            self.bank_i = 0
            self.S = S = Sched(nc)
            self.ident_f = A.tile(128, F32)
            self.perm64 = A.tile(128, F32)
            self.perm32 = A.tile(128, F32)
            self.ident_b = A.tile(128, BF16)
            self.ones_f = A.tile(128, F32)
            self.ones_b = A.tile(128, BF16)
            self.ropeg_c = A.tile(T, F32)
            self.ropeg_s = A.tile(T, F32)
            self.rope6_c = A.tile(T, F32)
            self.rope6_s = A.tile(T, F32)
            self.sel = A.tile(16, F32)
            self.tmpT = A.tile(128, F32)
            self.epsc = A.tile(1, F32)
            A.persist()
            for dst, nm in [(self.ident_f, "ident_f"), (self.perm64, "perm64"), (self.perm32, "perm32"),
                            (self.ropeg_c, "ropeg_c"), (self.ropeg_s, "ropeg_s"), (self.rope6_c, "rope6_c"),
                            (self.rope6_s, "rope6_s"), (self.sel, "sel")]:
                self.dma(dst, I[nm], [], ["const"], "const")
            self.dve(V("tensor_copy", out=self.ident_b, in_=self.ident_f), ["const"], ["const2"])
            self.dve(V("memset", self.ones_f, 1.0), [], ["const3"])
            self.dve(V("memset", self.ones_b, 1.0), [], ["const4"])
            self.dve(V("memset", self.epsc, EPS), [], ["const5"])
            self.dma(Sc["xres"][0:TL, :], I["x"], [], ["xres"], "xres")
            self.dma(Sc["xres"][TL:T, :], I["ctx"], [], ["xres"], "xres")
            S.barrier()
            phases = []
            phases.append(("gather", lambda: self.ph_gather()))
            phases.append(("mod", lambda: self.ph_mod()))
            for l in range(DEPTH):
                last = l == DEPTH - 1
                phases.append(("norm1_%d" % l, lambda l=l: self.ph_norm(l, 1, 10)))
                phases.append(("kv_%d" % l, lambda l=l: self.ph_kv(l)))
                phases.append(("q_%d" % l, lambda l=l, last=last: self.ph_q(l, 2 if last else 3)))
                phases.append(("att_%d" % l, lambda l=l, last=last: self.ph_att(l, 2 if last else 3)))
                phases.append(("conv_%d" % l, lambda l=l, last=last: self.ph_conv(l, not last)))
                phases.append(("merge_%d" % l, lambda l=l, last=last: self.ph_merge(l, 2 if last else 3)))
                phases.append(("out_%d" % l, lambda l=l, last=last: self.ph_out(l, 8 if last else 10)))
                phases.append(("norm2_%d" % l, lambda l=l, last=last: self.ph_norm(l, 2, 8 if last else 10)))
                phases.append(("ffn_%d" % l, lambda l=l, last=last: self.ph_ffn(l, not last)))
                phases.append(("down_%d" % l, lambda l=l, last=last: self.ph_down(l, not last)))
            phases.append(("final", lambda: self.ph_final(out)))
            for nm, fn in phases:
                A.reset()
                if nm.startswith("norm1_") or nm == "final":
                    S.epoch += 1
                fn()
                S.barrier()
                if self.stop_after == nm:
                    break
            S.finalize_and_emit()
        return nc

    def ph_gather(self):
        I, Sc = self.I, self.Sc
        self.Iext = dict(I)
        for nm, R, C in BIGW:
            if nm == "w_branch":
                I[nm] = Sc["full_" + nm].rearrange("l (n c) d -> l n c d", n=4)
            else:
                I[nm] = Sc["full_" + nm]
        self.gather_w([("w_in", 0)])

    def gather_w(self, lst):
        Sc = self.Sc
        big = {nm: (R, C) for nm, R, C in BIGW}
        for nm, l in lst:
            R, C = big[nm]
            rs = R // NCORE
            k = "sh_%s%d" % (nm, l)
            nsp = 4
            step = rs // nsp
            for q in range(nsp):
                self.S.op("pool", V("dma_start", out=Sc["sh_" + nm][l, q * step:(q + 1) * step, :],
                                    in_=self.Iext[nm][l, q * step:(q + 1) * step, :]),
                          reads=[], writes=[k], dma_key="shc", detached=True)
            self.S.op("pool", V("collective_compute", "AllGather", ALU.bypass,
                                replica_groups=[list(range(NCORE))], ins=[Sc["sh_" + nm][l]],
                                outs=[Sc["full_" + nm][l]]),
                      reads=[k], writes=["full_%s%d" % (nm, l)], dma_key="ccw", inc=1, detached=True)

    def ph_mod(self):
        A, S, I, Sc = self.A, self.S, self.I, self.Sc
        ccT = A.tile(32, F32)
        scT = A.tile(32, BF16)
        for r in range(2):
            self.dma(ccT.rearrange("p (k r) -> p k r", r=2)[:, :, r], I["cc"][r].rearrange("(k p) -> p k", p=128),
                     [], ["ccT"], "ccT", allow_slow_non_contiguous=True)
        self.act(scT, ccT, AF.Silu, ["ccT"], ["scT"])
        sc3 = scT.rearrange("p (k r) -> p k r", r=2)
        modsb = A.tile(MODC, F32, parts=2)
        bsb = A.tile(MODC, F32, parts=2)
        wbuf = [A.tile(16 * 512, BF16).rearrange("p (k n) -> p k n", k=16) for _ in range(2)]
        wi = 0
        for l in range(DEPTH):
            for r in range(2):
                self.dma(bsb[r:r + 1, :], I["b_mod"][l:l + 1, :], [], ["bsb"], "bsb")
            for cg in range(MODC // 512):
                w = wbuf[wi % 2]
                wk = "modw%d" % (wi % 2)
                wi += 1
                self.wload(w, I["w_mod"][l][:, cg * 512:(cg + 1) * 512], wk, 16, cast=True)
                b = self.bank()
                for k in range(16):
                    self.mm(b, self.ps(b, 512, parts=2), sc3[:, k, :], w[:, k, :], k == 0, k == 15, ["scT", wk])
                self.dve(V("tensor_tensor", out=modsb[:, cg * 512:(cg + 1) * 512], in0=self.ps(b, 512, parts=2),
                           in1=bsb[:, cg * 512:(cg + 1) * 512], op=ALU.add),
                         ["bsb"], ["ps%d" % b, "modsb"])
            self.dma(Sc["modp"][2 * l:2 * l + 2, :], modsb, ["modsb"], ["modp"], "modp")
        self.S.op("pool", V("collective_compute", "AllGather", ALU.bypass, replica_groups=[list(range(NCORE))],
                            ins=[Sc["modp"]], outs=[Sc["modg"]]),
                  reads=["modp"], writes=["modg"], dma_key="cc", inc=1)
        mg = Sc["modg"].rearrange("(r q) c -> q r c", q=2 * DEPTH)
        for l in range(DEPTH):
            for s_ in range(2):
                self.dma(Sc["modv"][l, s_].rearrange("(r c) -> r c", c=MODC), mg[2 * l + s_], ["modg"], ["modv"], "modv")

    def bcast_load(self, dst, src_row, key):
        self.dma(dst, src_row.partition_broadcast(128), [], [key], key)

    def ph_norm(self, l, which, ntiles, final_out=None):
        A, S, I, Sc = self.A, self.S, self.I, self.Sc
        ncols = ntiles * 128
        hT_sb = A.tile(16 * ncols, BF16).rearrange("p (k t) -> p k t", k=16)
        AB = {}
        gn = A.tile(D, F32)
        if final_out is None:
            self.bcast_load(gn, (I["norm1_g"] if which == 1 else I["norm2_g"])[l], "gn")
            off_sh = 0 if which == 1 else 3 * D
            off_sc = off_sh + D
            for r in range(2 if ntiles > 8 else 1):
                a_t = A.tile(D, F32)
                b_t = A.tile(D, F32)
                self.bcast_load(a_t, Sc["modv"][l, r, off_sc:off_sc + D], "A%d" % r)
                self.bcast_load(b_t, Sc["modv"][l, r, off_sh:off_sh + D], "B%d" % r)
                self.dve(V("scalar_tensor_tensor", out=a_t, in0=a_t, scalar=1.0, in1=gn,
                                                                     op0=ALU.add, op1=ALU.mult),
                         ["gn", "A%d" % r], ["A%d" % r])
                AB[r] = (a_t, b_t)
        else:
            self.bcast_load(gn, I["final_g"], "gn")
        xt = [A.tile(D, F32) for _ in range(2)]
        hf = A.tile(D, F32)
        hb = [A.tile(D, BF16) for _ in range(2)]
        ss = A.tile(4, F32)
        for tt in range(ntiles):
            x_ = xt[tt % 2]
            xk = "xt%d" % (tt % 2)
            r = 0 if tt < 8 else 1
            self.dma(x_, Sc["xres"][tt * 128:(tt + 1) * 128, :], ["xres"], [xk], xk)
            self.dve(V("memset", ss[:, 0:1], 0.0), [], ["ss"])
            self.act(hf, x_, AF.Square, [xk], ["hf", "ss"], accum_out=ss[:, 0:1])
            self.act(ss[:, 1:2], ss[:, 0:1], AF.Sqrt, ["ss"], ["ss1"], scale=1.0 / D, bias=self.epsc[:, 0:1])
            self.dve(V("reciprocal", out=ss[:, 2:3], in_=ss[:, 1:2]), ["ss1"], ["ss2"])
            if final_out is not None:
                self.dve(V("scalar_tensor_tensor", out=hf, in0=x_, scalar=ss[:, 2:3], in1=gn,
                                                                   op0=ALU.mult, op1=ALU.mult),
                         [xk, "ss2", "gn"], ["hf"])
                self.dma(final_out[tt * 128:(tt + 1) * 128, :], hf, ["hf"], ["outf"], "outf")
                continue
            a_t, b_t = AB[r]
            h_ = hb[tt % 2]
            hk = "hb%d" % (tt % 2)
            self.dve(V("scalar_tensor_tensor", out=hf, in0=x_, scalar=ss[:, 2:3], in1=a_t,
                                                                        op0=ALU.mult, op1=ALU.mult),
                     [xk, "ss2", "A%d" % r], ["hf"])
            self.dve(V("tensor_tensor", out=h_, in0=hf, in1=b_t, op=ALU.add),
                     ["hf", "B%d" % r], [hk])
            for half in range(2):
                b = self.bank()
                pb = self.ps(b, 1024, dt=BF16)
                for j in range(8):
                    k = half * 8 + j
                    self.S.op("pe", V("transpose",
                        pb[:, j * 128:(j + 1) * 128], h_[:, k * 128:(k + 1) * 128], self.ident_b),
                        reads=[hk, "const2"], writes=["ps%d" % b])
                dst = hT_sb[:, half * 8:(half + 1) * 8, tt * 128:(tt + 1) * 128]
                src = pb.rearrange("p (j t) -> p j t", j=8)
                if half == 0:
                    self.act(dst, src, AF.Copy, [], ["ps%d" % b, "hT_sb"])
                else:
                    self.dve(V("tensor_copy", out=dst, in_=src), [], ["ps%d" % b, "hT_sb"])
        if final_out is not None:
            return
        self.dma(Sc["hT"][:, :, 0:ncols].rearrange("k p t -> p k t"), hT_sb, ["hT_sb"], ["hT"], "hT")
        if which == 2:
            self.dma(Sc["pay_h2"][:, 0:1].rearrange("(k p) o -> p k o", p=128), hT_sb[:, :, 0:1],
                     ["hT_sb"], ["pay_h2"], "pay_h2")
            self.dma(Sc["pay_h2"][:, 1:2].rearrange("(k p) o -> p k o", p=128), hT_sb[:, :, TL - 1:TL],
                     ["hT_sb"], ["pay_h2"], "pay_h2")
            self.S.op("pool", V("collective_compute", "AllGather", ALU.bypass,
                                                             replica_groups=[list(range(NCORE))],
                                                             ins=[Sc["pay_h2"]], outs=[Sc["gh2"]]),
                      reads=["pay_h2"], writes=["gh2"], dma_key="cc", inc=1)

    def rstd_from_sum(self, b, n, cnt, rkey):
        r = self.rs_t[self.rs_i % 2][:, 0:n]
        k = "rs%d" % (self.rs_i % 2)
        self.rs_i += 1
        self.act(r, self.ps(b, n), AF.Sqrt, [], ["ps%d" % b, k], scale=1.0 / cnt, bias=self.epsc[:, 0:1])
        self.dve(V("reciprocal", out=r, in_=r), [k], [k])
        return r, k

    def rope(self, xf, xk, n, c0, kind, parts=128):
        perm = self.perm64 if kind == "g" else self.perm32
        cs = (self.ropeg_c if kind == "g" else self.rope6_c)[0:parts, c0:c0 + n]
        sn = (self.ropeg_s if kind == "g" else self.rope6_s)[0:parts, c0:c0 + n]
        b = self.bank()
        self.mm(b, self.ps(b, n, parts=parts), perm[0:parts, 0:parts], xf, True, True, [xk, "const"])
        t2 = self.rp_t[0:parts, 0:n]
        self.dve(V("tensor_tensor", out=t2, in0=self.ps(b, n, parts=parts), in1=sn, op=ALU.mult),
                 ["const"], ["ps%d" % b, "rp_t"])
        self.dve(V("tensor_tensor", out=xf, in0=xf, in1=cs, op=ALU.mult), [xk, "const"], [xk])
        self.dve(V("tensor_tensor", out=xf, in0=xf, in1=t2, op=ALU.add), [xk, "rp_t"], [xk])

    def fm_tiles(self, w3, wk, hT_sb, col0, M, g0, gn):
        b = self.bank()
        for k in range(16):
            self.mm(b, self.ps(b, gn, parts=M), w3[:, k, col0:col0 + M], hT_sb[:, k, g0:g0 + gn],
                    k == 0, k == 15, [wk, "hT_sb"])
        return b

    def alloc_fm_scratch(self):
        A = self.A
        self.rs_t = [A.tile(512, F32) for _ in range(2)]
        self.rs_i = 0
        self.rp_t = A.tile(512, F32)
        self.xf_t = [A.tile(512, F32) for _ in range(6)]
        self.xf_i = 0
        self.sq_t = A.tile(512, F32)
        self.st_t = [A.tile(512, BF16) for _ in range(4)]
        self.st_i = 0

    def xf(self):
        i = self.xf_i % 6
        self.xf_i += 1
        return self.xf_t[i], "xf%d" % i

    def stg(self):
        i = self.st_i % 4
        self.st_i += 1
        return self.st_t[i], "st%d" % i

    def store_fm(self, src, sk, parts, n, g0, dst_lat, dst_ctx, row0, dkey):
        st, stk = self.stg()
        self.act(st[0:parts, 0:n], src, AF.Copy, [sk], [stk])
        if g0 < TL:
            self.dma(dst_lat[row0:row0 + parts, g0:g0 + n], st[0:parts, 0:n], [stk], [dkey], stk)
        else:
            self.dma(dst_ctx[row0:row0 + parts, g0 - TL:g0 - TL + n], st[0:parts, 0:n], [stk], [dkey], stk)

    def load_hT(self, ncols):
        hT_sb = self.A.tile(16 * ncols, BF16).rearrange("p (k t) -> p k t", k=16)
        self.dma(hT_sb, self.Sc["hT"][:, :, 0:ncols].rearrange("k p t -> p k t"), ["hT"], ["hT_sb"], "hT_sb")
        return hT_sb

    def normed_head(self, b, gain, gk, n, g0, rope_kind):
        x_, xk = self.xf()
        x_ = x_[:, 0:n]
        self.act(x_, self.ps(b, n), AF.Copy, [gk], ["ps%d" % b, xk], scale=gain)
        self.act(self.sq_t[:, 0:n], self.ps(b, n), AF.Square, [], ["ps%d" % b, "sq_t"])
        b2 = self.bank()
        self.mm(b2, self.ps(b2, n), self.ones_f, self.sq_t[:, 0:n], True, True, ["sq_t", "const3"])
        r, rk = self.rstd_from_sum(b2, n, 128.0, None)
        self.rope(x_, xk, n, g0, rope_kind)
        self.dve(V("tensor_tensor", out=x_, in0=x_, in1=r, op=ALU.mult), [xk, rk], [xk])
        return x_, xk

    def ph_kv(self, l):
        A, S, I, Sc = self.A, self.S, self.I, self.Sc
        hT_sb = self.load_hT(T)
        self.alloc_fm_scratch()
        wbuf = [A.tile(16 * 512, BF16).rearrange("p (k n) -> p k n", k=16) for _ in range(2)]
        wi = [0]

        def wnext(c0, ncols):
            i = wi[0] % 2
            wi[0] += 1
            wk = "w%d" % i
            self.wload(wbuf[i][:, :, 0:ncols], I["w_in"][l][:, c0:c0 + ncols], wk, 16, rkey="full_w_in%d" % l)
            return wbuf[i], wk

        gkg = A.tile(1, F32)
        self.load_T(gkg, I["gqa_k_g"][l:l + 1, :], 1, None, "gkg")
        gkv = A.tile(2, F32)
        self.load_T(gkv, I["mla_kv_g"][l].rearrange("(c p) -> c p", p=128), 2, None, "gkv")
        wkv = A.tile(2 * 1024, BF16).rearrange("p (k n) -> p k n", k=2)
        self.wload(wkv, I["mla_w_kv_up"][l], "wkv", 2, split=1, cast=True)
        kvn = A.tile(2 * T, BF16).rearrange("p (k t) -> p k t", k=2)
        vst = [A.tile(512, BF16) for _ in range(2)]

        def v_tokmajor(lhs3, lk, nk, rhs_fn, rk, vcol0, out_re=None):
            for tt in range(10):
                b = self.bank()
                for k in range(nk):
                    o_ = self.ps(b, 512)
                    if out_re is not None:
                        o_ = o_.rearrange("p (h d) -> p h d", h=4)
                    self.mm(b, o_, lhs3[:, k, tt * 128:(tt + 1) * 128], rhs_fn(k), k == 0, k == nk - 1,
                            [lk, rk])
                st = vst[tt % 2]
                sk = "vst%d" % (tt % 2)
                self.act(st, self.ps(b, 512), AF.Copy, [], ["ps%d" % b, sk])
                if tt < 8:
                    self.dma(Sc["pay_v"][tt * 128:(tt + 1) * 128, vcol0:vcol0 + 512], st, [sk], ["pay_v"], sk)
                else:
                    self.dma(Sc["vc"][(tt - 8) * 128:(tt - 7) * 128, vcol0:vcol0 + 512], st, [sk], ["vc"], sk)

        w, wk = wnext(0, 512)
        for j in range(2):
            for (g0, gn) in GROUPS:
                b = self.fm_tiles(w, wk, hT_sb, j * 128, 128, g0, gn)
                x_, xk = self.normed_head(b, gkg[:, 0:1], "gkg", gn, g0, "g")
                self.store_fm(x_, xk, 128, gn, g0, Sc["pay_k"], Sc["kc"], j * 128, "pay_k")
        for tt in range(10):
            b = self.bank()
            for k in range(16):
                self.mm(b, self.ps(b, 256), hT_sb[:, k, tt * 128:(tt + 1) * 128], w[:, k, 256:512], k == 0, k == 15,
                        ["hT_sb", wk])
            st = vst[tt % 2]
            sk = "vst%d" % (tt % 2)
            self.act(st[:, 0:256], self.ps(b, 256), AF.Copy, [], ["ps%d" % b, sk])
            if tt < 8:
                self.dma(Sc["pay_v"][tt * 128:(tt + 1) * 128, 0:256], st[:, 0:256], [sk], ["pay_v"], sk)
            else:
                self.dma(Sc["vc"][(tt - 8) * 128:(tt - 7) * 128, 0:256], st[:, 0:256], [sk], ["vc"], sk)
        w, wk = wnext(512, 320)
        for (g0, gn) in GROUPS:
            bs = [self.fm_tiles(w, wk, hT_sb, c * 128, 128, g0, gn) for c in range(2)]
            b2 = self.bank()
            xs = []
            for c in range(2):
                x_, xk = self.xf()
                x_ = x_[:, 0:gn]
                self.act(x_, self.ps(bs[c], gn), AF.Copy, ["gkv"], ["ps%d" % bs[c], xk], scale=gkv[:, c:c + 1])
                self.act(self.sq_t[:, 0:gn], self.ps(bs[c], gn), AF.Square, [], ["ps%d" % bs[c], "sq_t"])
                self.mm(b2, self.ps(b2, gn), self.ones_f, self.sq_t[:, 0:gn], c == 0, c == 1, ["sq_t", "const3"])
                xs.append((x_, xk))
            r, rk = self.rstd_from_sum(b2, gn, 256.0, None)
            for c in range(2):
                x_, xk = xs[c]
                self.dve(V("tensor_tensor", out=kvn[:, c, g0:g0 + gn], in0=x_, in1=r, op=ALU.mult),
                         [xk, rk], ["kvn"])
            b = self.fm_tiles(w, wk, hT_sb, 256, 64, g0, gn)
            x_, xk = self.xf()
            x_ = x_[0:64, 0:gn]
            self.act(x_, self.ps(b, gn, parts=64), AF.Copy, [], ["ps%d" % b, xk])
            self.rope(x_, xk, gn, g0, "6", parts=64)
            self.store_fm(x_, xk, 64, gn, g0, Sc["pay_k"], Sc["kc"], 768, "pay_k")
        for h in range(4):
            for (g0, gn) in GROUPS:
                b = self.bank()
                for k in range(2):
                    self.mm(b, self.ps(b, gn), wkv[:, k, h * 256:h * 256 + 128], kvn[:, k, g0:g0 + gn], k == 0, k == 1,
                            ["wkv", "kvn"])
                x_, xk = self.xf()
                x_ = x_[:, 0:gn]
                self.act(x_, self.ps(b, gn), AF.Copy, [], ["ps%d" % b, xk])
                self.store_fm(x_, xk, 128, gn, g0, Sc["pay_k"], Sc["kc"], 256 + h * 128, "pay_k")
        wkv_v = wkv.rearrange("p k (h two d) -> p k h two d", two=2, d=128)
        v_tokmajor(kvn, "kvn", 2, lambda k: wkv_v[:, k, :, 1, :], "wkv", 256, out_re=True)
        w, wk = wnext(832, 512)
        for h in range(4):
            for (g0, gn) in GROUPS:
                b = self.fm_tiles(w, wk, hT_sb, h * 128, 128, g0, gn)
                x_, xk = self.xf()
                x_ = x_[:, 0:gn]
                self.act(x_, self.ps(b, gn), AF.Copy, [], ["ps%d" % b, xk])
                self.rope(x_, xk, gn, g0, "6")
                self.store_fm(x_, xk, 128, gn, g0, Sc["pay_k"], Sc["kc"], 832 + h * 128, "pay_k")
        w, wk = wnext(1344, 512)
        v_tokmajor(hT_sb, "hT_sb", 16, lambda k: w[:, k, :], wk, 768)
        wa, wak = wnext(N_KV, 512)
        wg, wgk = wnext(N_KV + 512, 512)
        sst = [A.tile(512, F32) for _ in range(2)]
        for c in range(4):
            for gi, (g0, gn) in enumerate(GROUPS):
                ba = self.fm_tiles(wa, wak, hT_sb, c * 128, 128, g0, gn)
                bg = self.fm_tiles(wg, wgk, hT_sb, c * 128, 128, g0, gn)
                x_, xk = self.xf()
                x_ = x_[:, 0:gn]
                self.act(x_, self.ps(bg, gn), AF.Sigmoid, [], ["ps%d" % bg, xk])
                i = (c * 3 + gi) % 2
                s_ = sst[i][:, 0:gn]
                sk = "sst%d" % i
                self.dve(V("tensor_tensor", out=s_, in0=self.ps(ba, gn), in1=x_,
                                                                                op=ALU.mult),
                         [xk], ["ps%d" % ba, sk])
                self.dma(Sc["sT"][c * 128:(c + 1) * 128, g0:g0 + gn], s_, [sk], ["sT"], sk)
                if gi == 0:
                    self.dma(Sc["pay_h"][c * 128:(c + 1) * 128, 0:15], s_[:, 0:15], [sk], ["pay_h"], sk)
                if gi == 1:
                    self.dma(Sc["pay_h"][c * 128:(c + 1) * 128, 15:30], s_[:, 497:512], [sk], ["pay_h"], sk)
        for src, dst, rk in [("pay_k", "gk", "pay_k"), ("pay_v", "gv", "pay_v"), ("pay_h", "gh", "pay_h")]:
            self.S.op("pool", V("collective_compute",
                "AllGather", ALU.bypass, replica_groups=[list(range(NCORE))], ins=[Sc[src]], outs=[Sc[dst]]),
                reads=[rk], writes=[dst], dma_key="cc", inc=1)

    def ph_q(self, l, ngroups):
        A, S, I, Sc = self.A, self.S, self.I, self.Sc
        groups = GROUPS[:ngroups]
        ncols = T if ngroups == 3 else TL
        hT_sb = self.load_hT(ncols)
        self.alloc_fm_scratch()
        wbuf = [A.tile(16 * 512, BF16).rearrange("p (k n) -> p k n", k=16) for _ in range(2)]
        gqg = A.tile(1, F32)
        self.load_T(gqg, I["gqa_q_g"][l:l + 1, :], 1, None, "gqg")
        gql = A.tile(4, F32)
        self.load_T(gql, I["mla_q_g"][l].rearrange("(c p) -> c p", p=128), 4, None, "gql")
        wqn = A.tile(4 * 512, BF16).rearrange("p (k n) -> p k n", k=4)
        wqr = A.tile(4 * 256, BF16).rearrange("p (k n) -> p k n", k=4)
        wq_src = I["mla_w_q_up"][l].rearrange("(k p) (h d) -> p k h d", p=128, d=192)
        for k in range(4):
            self.dma(wqn[:, k, :].rearrange("p (h d) -> p h d", d=128), wq_src[:, k, :, 0:128], [], ["wqn"], "wqn",
                     eng="pool")
            self.dma(wqr[:, k, :].rearrange("p (h d) -> p h d", d=64), wq_src[:, k, :, 128:192], [], ["wqr"], "wqr",
                     eng="pool")
        qn = A.tile(4 * ncols, BF16).rearrange("p (k t) -> p k t", k=4)
        w, wk = wbuf[0], "w0"
        self.wload(w, I["w_in"][l][:, N_KV + 1024:N_KV + 1536], wk, 16, rkey="full_w_in%d" % l)
        for h in range(4):
            for (g0, gn) in groups:
                b = self.fm_tiles(w, wk, hT_sb, h * 128, 128, g0, gn)
                x_, xk = self.normed_head(b, gqg[:, 0:1], "gqg", gn, g0, "g")
                self.store_fm(x_, xk, 128, gn, 0, Sc["qT"][:, g0:g0 + gn], None, h * 128, "qT")
        w, wk = wbuf[1], "w1"
        self.wload(w, I["w_in"][l][:, N_KV + 1536:N_KV + 2048], wk, 16, rkey="full_w_in%d" % l)
        for (g0, gn) in groups:
            bs = [self.fm_tiles(w, wk, hT_sb, c * 128, 128, g0, gn) for c in range(4)]
            b2 = self.bank()
            xs = []
            for c in range(4):
                x_, xk = self.xf()
                x_ = x_[:, 0:gn]
                self.act(x_, self.ps(bs[c], gn), AF.Copy, ["gql"], ["ps%d" % bs[c], xk], scale=gql[:, c:c + 1])
                self.act(self.sq_t[:, 0:gn], self.ps(bs[c], gn), AF.Square, [], ["ps%d" % bs[c], "sq_t"])
                self.mm(b2, self.ps(b2, gn), self.ones_f, self.sq_t[:, 0:gn], c == 0, c == 3, ["sq_t", "const3"])
                xs.append((x_, xk))
            r, rk = self.rstd_from_sum(b2, gn, 512.0, None)
            for c in range(4):
                x_, xk = xs[c]
                self.dve(V("tensor_tensor", out=qn[:, c, g0:g0 + gn], in0=x_, in1=r,
                                                                              op=ALU.mult),
                         [xk, rk], ["qn"])
        for h in range(4):
            for (g0, gn) in groups:
                b = self.bank()
                for k in range(4):
                    self.mm(b, self.ps(b, gn), wqn[:, k, h * 128:(h + 1) * 128], qn[:, k, g0:g0 + gn], k == 0, k == 3,
                            ["wqn", "qn"])
                x_, xk = self.xf()
                x_ = x_[:, 0:gn]
                self.act(x_, self.ps(b, gn), AF.Copy, [], ["ps%d" % b, xk])
                self.store_fm(x_, xk, 128, gn, 0, Sc["qT"][:, g0:g0 + gn], None, 512 + h * 128, "qT")
        for pr in range(2):
            for (g0, gn) in groups:
                b = self.bank()
                for k in range(4):
                    self.mm(b, self.ps(b, gn), wqr[:, k, pr * 128:(pr + 1) * 128], qn[:, k, g0:g0 + gn], k == 0, k == 3,
                            ["wqr", "qn"])
                x_, xk = self.xf()
                x_ = x_[:, 0:gn]
                self.act(x_, self.ps(b, gn), AF.Copy, [], ["ps%d" % b, xk])
                self.rope(x_, xk, gn, g0, "6")
                self.store_fm(x_, xk, 128, gn, 0, Sc["qT"][:, g0:g0 + gn], None, 1024 + pr * 128, "qT")
        w, wk = wbuf[0], "w0"
        self.wload(w, I["w_in"][l][:, N_KV + 2048:N_KV + 2560], wk, 16, rkey="full_w_in%d" % l)
        for h in range(4):
            for (g0, gn) in groups:
                b = self.fm_tiles(w, wk, hT_sb, h * 128, 128, g0, gn)
                x_, xk = self.xf()
                x_ = x_[:, 0:gn]
                self.act(x_, self.ps(b, gn), AF.Copy, [], ["ps%d" % b, xk])
                self.rope(x_, xk, gn, g0, "6")
                self.store_fm(x_, xk, 128, gn, 0, Sc["qT"][:, g0:g0 + gn], None, 1280 + h * 128, "qT")

    def ph_att(self, l, ngroups):
        A, S, I, Sc = self.A, self.S, self.I, self.Sc
        if l == 0:
            self.gather_w([("w_branch", 0), ("w_out", 0), ("ffn_w_up", 0), ("ffn_w_down", 0), ("w_in", 1),
                           ("w_branch", 1), ("w_out", 1), ("ffn_w_up", 1), ("ffn_w_down", 1)])
        groups = GROUPS[:ngroups]
        lam_init = 0.8 - 0.6 * math.exp(-0.3 * l)
        lt = A.tile(4 * 64, F32).rearrange("p (a d) -> p a d", a=4)
        for i, nm in enumerate(["diff_lq1", "diff_lk1", "diff_lq2", "diff_lk2"]):
            self.dma(lt[:, i, :], I[nm][l].partition_broadcast(128), [], ["lt"], "lt")
        lw = A.tile(8, F32)
        lp = A.tile(64, F32)
        for j in range(2):
            self.dve(V("tensor_tensor", out=lp, in0=lt[:, 2 * j, :], in1=lt[:, 2 * j + 1, :], op=ALU.mult),
                     ["lt"], ["lp"])
            self.dve(V("reduce_sum", out=lw[:, j:j + 1], in_=lp, axis=AX.X), ["lp"], ["lw%d" % j])
            self.act(lw[:, 2 + j:3 + j], lw[:, j:j + 1], AF.Exp, ["lw%d" % j], ["le%d" % j])
        self.dve(V("tensor_tensor", out=lw[:, 4:5], in0=lw[:, 3:4], in1=lw[:, 2:3], op=ALU.subtract),
                 ["le0", "le1"], ["nl0"])
        self.dve(V("tensor_scalar_add", out=lw[:, 5:6], in0=lw[:, 4:5], scalar1=-lam_init), ["nl0"], ["nlam"])
        nlam = lw[:, 5:6]
        gd = A.tile(2, F32)
        self.load_T(gd[:, 0:1], I["diff_g"][l:l + 1, :], 1, None, "gd0")
        self.dve(V("tensor_scalar_mul", out=gd[:, 1:2], in0=gd[:, 0:1], scalar1=1.0 - lam_init), ["gd0"], ["gd"])
        self.rs_t = [A.tile(512, F32) for _ in range(2)]
        self.rs_i = 0
        self.sq_t = A.tile(512, F32)
        rec = [A.tile(512, F32) for _ in range(2)]
        of = [A.tile(512, F32) for _ in range(2)]
        ost = [A.tile(512, BF16) for _ in range(2)]
        qb = [A.tile(512, BF16) for _ in range(4)]
        kmain = [A.tile(1024, BF16) for _ in range(2)]
        krope = [A.tile(1024, BF16) for _ in range(2)]
        vp = [A.tile(8 * 128, BF16).rearrange("p (c d) -> p c d", c=8) for _ in range(2)]
        NPT = 6
        LOOK = 2
        pt = [A.tile(512, BF16) for _ in range(NPT)]
        acc = [[A.tile(512, F32) for _ in range(2)] for _ in range(2)]
        pi = [0]
        pci = [0]
        ost_i = [0]

        jobs = []
        for kvh in range(2):
            jobs.append(dict(kind="gqa", krow=kvh * 128, vcol=kvh * 128, scale=128 ** -0.5,
                             streams=[dict(q=(2 * kvh + s) * 128, out=512 + (2 * kvh + s) * 128) for s in range(2)]))
        for h in range(4):
            jobs.append(dict(kind="mla", krow=256 + h * 128, vcol=256 + h * 128, scale=192 ** -0.5,
                             streams=[dict(q=512 + h * 128, qr=1024 + (h // 2) * 128 + (h % 2) * 64,
                                           out=1024 + h * 128)]))
        for h in range(4):
            jobs.append(dict(kind="diff", krow=832 + h * 128, vcol=768 + h * 128, scale=64 ** -0.5,
                             streams=[dict(q=1280 + h * 128, half=0), dict(q=1280 + h * 128, half=1)],
                             out=1536 + h * 128))

        for job in jobs:
            kind = job["kind"]
            ns = len(job["streams"])
            for (g0, gn) in groups:
                qk = []
                for si, st in enumerate(job["streams"]):
                    if kind == "diff" and si == 1:
                        qk.append(qk[0])
                        continue
                    qt = qb[2 * si]
                    k_ = "qb%d" % (2 * si)
                    self.dma(qt[:, 0:gn], Sc["qT"][st["q"]:st["q"] + 128, g0:g0 + gn], ["qT"], [k_], k_)
                    ent = [(qt, k_)]
                    if kind == "mla":
                        qr = qb[2 * si + 1]
                        k2 = "qb%d" % (2 * si + 1)
                        self.dma(qr[0:64, 0:gn], Sc["qT"][st["qr"]:st["qr"] + 64, g0:g0 + gn], ["qT"], [k2], k2)
                        ent.append((qr, k2))
                    qk.append(ent)
                ob = [self.bank() for _ in range(ns)]
                sb = []
                pieces = [("ctx", 0, 2)] if g0 >= TL else [("ctx", 0, 2)] + [("lat", r, 8) for r in range(NCORE)]
                seq = []
                for pidx, (src, r, nch) in enumerate(pieces):
                    for c in range(nch):
                        for si in range(ns):
                            seq.append((pidx, c, si))
                n_items = len(seq)
                last_idx = {si: max(i for i, it in enumerate(seq) if it[2] == si) for si in range(ns)}
                loaded = {}

                def ensure(pidx):
                    if pidx in loaded or pidx >= len(pieces):
                        return
                    src, r, nch = pieces[pidx]
                    i = pci[0] % 2
                    pci[0] += 1
                    km, kr, v_ = kmain[i], krope[i], vp[i]
                    kk = "kp%d" % i
                    nkeys = nch * 128
                    if src == "ctx":
                        ksrc, vsrc, r0, v0 = Sc["kc"], Sc["vc"], 0, 0
                    else:
                        ksrc, vsrc, r0, v0 = Sc["gk"], Sc["gv"], r * KROWS, r * TL
                    rk_ = ["kc", "vc"] if src == "ctx" else ["gk", "gv"]
                    self.dma(km[:, 0:nkeys], ksrc[r0 + job["krow"]:r0 + job["krow"] + 128, 0:nkeys], rk_, [kk], kk)
                    if kind == "mla":
                        self.dma(kr[0:64, 0:nkeys], ksrc[r0 + 768:r0 + 832, 0:nkeys], rk_, [kk], kk)
                    self.dma(v_[:, 0:nch, :],
                             vsrc[v0:v0 + nkeys, job["vcol"]:job["vcol"] + 128].rearrange("(c p) d -> p c d", p=128),
                             rk_, [kk], kk)
                    loaded[pidx] = (km, kr, v_, kk)

                sbanks = {}

                def emit_s(ii):
                    pidx, c, si = seq[ii]
                    km, kr, v_, kk = loaded[pidx]
                    b = self.bank()
                    while b in ob or b in sb:
                        b = self.bank()
                    sbanks[ii] = b
                    ent = qk[si]
                    if kind == "gqa":
                        self.mm(b, self.ps(b, gn), km[:, c * 128:(c + 1) * 128], ent[0][0][:, 0:gn], True, True,
                                [kk, ent[0][1]])
                    elif kind == "mla":
                        self.mm(b, self.ps(b, gn), km[:, c * 128:(c + 1) * 128], ent[0][0][:, 0:gn], True, False,
                                [kk, ent[0][1]])
                        self.mm(b, self.ps(b, gn), kr[0:64, c * 128:(c + 1) * 128], ent[1][0][0:64, 0:gn], False, True,
                                [kk, ent[1][1]])
                    else:
                        hf_ = job["streams"][si]["half"]
                        self.mm(b, self.ps(b, gn), km[hf_ * 64:(hf_ + 1) * 64, c * 128:(c + 1) * 128],
                                ent[0][0][hf_ * 64:(hf_ + 1) * 64, 0:gn], True, True, [kk, ent[0][1]])

                first = [True] * ns
                ensure(0)
                ensure(1)
                nxt = [0]

                def pump(upto):
                    while nxt[0] <= min(upto, n_items - 1):
                        emit_s(nxt[0])
                        nxt[0] += 1

                cur_piece = 0
                cnt = [0] * ns
                for ii in range(n_items):
                    pidx, c, si = seq[ii]
                    if pidx != cur_piece:
                        cur_piece = pidx
                        ensure(pidx + 1)
                    pump(ii + LOOK)
                    km, kr, v_, kk = loaded[pidx]
                    b = sbanks.pop(ii)
                    p_ = pt[pi[0] % NPT]
                    pk = "pt%d" % (pi[0] % NPT)
                    pi[0] += 1
                    self.act(p_[:, 0:gn], self.ps(b, gn), AF.Exp, [], ["ps%d" % b, pk], scale=job["scale"])
                    last_ = last_idx[si] == ii
                    self.mm(ob[si], self.ps(ob[si], gn), v_[:, c, :], p_[:, 0:gn], first[si], last_, [kk, pk])
                    first[si] = False
                    par = cnt[si] % 2
                    a_ = acc[si][par][:, 0:gn]
                    ak = "acc%d_%d" % (si, par)
                    if cnt[si] < 2:
                        self.dve(V("tensor_copy", out=a_, in_=p_[:, 0:gn]), [pk], [ak])
                    else:
                        self.dve(V("tensor_tensor", out=a_, in0=a_, in1=p_[:, 0:gn], op=ALU.add), [pk, ak], [ak])
                    cnt[si] += 1
                for si in range(ns):
                    a0 = acc[si][0][:, 0:gn]
                    if cnt[si] > 1:
                        self.dve(V("tensor_tensor", out=a0, in0=a0, in1=acc[si][1][:, 0:gn], op=ALU.add),
                                 ["acc%d_0" % si, "acc%d_1" % si], ["acc%d_0" % si])
                    b = self.bank()
                    while b in ob or b in sb:
                        b = self.bank()
                    sb.append(b)
                    self.mm(b, self.ps(b, gn), self.ones_f, a0, True, True, ["acc%d_0" % si, "const3"])
                outs = []
                for si in range(ns):
                    r_ = rec[si][:, 0:gn]
                    rk = "rec%d" % si
                    self.dve(V("reciprocal", out=r_, in_=self.ps(sb[si], gn)), [],
                             ["ps%d" % sb[si], rk])
                    o_ = of[si][:, 0:gn]
                    ok = "of%d" % si
                    self.dve(V("tensor_tensor", out=o_, in0=self.ps(ob[si], gn), in1=r_,
                                                                            op=ALU.mult),
                             [rk], ["ps%d" % ob[si], ok])
                    outs.append((o_, ok))
                if kind == "diff":
                    o1, k1 = outs[0]
                    o2, k2 = outs[1]
                    self.dve(V("scalar_tensor_tensor", out=o1, in0=o2, scalar=nlam, in1=o1,
                                                                            op0=ALU.mult, op1=ALU.add),
                             [k1, k2, "nlam"], [k1])
                    self.act(self.sq_t[:, 0:gn], o1, AF.Square, [k1], ["sq_t"])
                    b2 = self.bank()
                    self.mm(b2, self.ps(b2, gn), self.ones_f, self.sq_t[:, 0:gn], True, True, ["sq_t", "const3"])
                    r, rk = self.rstd_from_sum(b2, gn, 128.0, None)
                    self.dve(V("scalar_tensor_tensor", out=o1, in0=o1, scalar=gd[:, 1:2], in1=r,
                                                                          op0=ALU.mult, op1=ALU.mult),
                             [k1, rk, "gd"], [k1])
                    fin = [(o1, k1, job["out"])]
                else:
                    fin = [(outs[si][0], outs[si][1], job["streams"][si]["out"]) for si in range(ns)]
                for (o_, ok, orow) in fin:
                    st = ost[ost_i[0] % 2]
                    sk = "ost%d" % (ost_i[0] % 2)
                    ost_i[0] += 1
                    self.act(st[:, 0:gn], o_, AF.Copy, [ok], [sk])
                    self.dma(Sc["brT"][orow:orow + 128, g0:g0 + gn], st[:, 0:gn], [sk], ["brT"], sk)

    def ph_conv(self, l, with_ctx):
        A, S, I, Sc = self.A, self.S, self.I, self.Sc
        cw = A.tile(4 * 31, F32).rearrange("p (c k) -> p c k", c=4)
        for c in range(4):
            self.load_T(cw[:, c, :], I["conv_w"][l][:, c * 128:(c + 1) * 128], 31, None, "cw")
        cv = A.tile(12, F32).rearrange("p (a c) -> p a c", a=3)
        for i, nm in enumerate(["conv_b", "conv_ln_g", "conv_ln_b"]):
            self.load_T(cv[:, i, :], I[nm][l].rearrange("(c p) -> c p", p=128), 4, None, "cv")
        segs = [(0, TL)] + ([(TL, TC)] if with_ctx else [])
        NE = TL + 30
        s_ext = A.tile(4 * NE, F32).rearrange("p (c t) -> p c t", c=4)
        ghs = A.tile(4 * 8 * 30, F32).rearrange("p (c r j) -> p c r j", c=4, r=8)
        u = A.tile(4 * TL, F32).rearrange("p (c t) -> p c t", c=4)
        sq = A.tile(512, F32)
        mean = A.tile(512, F32)
        msq = A.tile(512, F32)
        rstd = A.tile(512, F32)
        tt_ = A.tile(512, F32)
        yst = [A.tile(512, BF16) for _ in range(2)]
        yi = 0
        for (t0, n) in segs:
            ne = n + 30
            if t0 == 0:
                for c in range(4):
                    self.dma(ghs[:, c, :, :], Sc["gh"].rearrange("(r c p) j -> c p r j", c=4, p=128)[c], ["gh"], ["ghs"],
                             "ghs")
                for side in range(2):
                    dst = s_ext[:, :, 0:15] if side == 0 else s_ext[:, :, 15 + n:30 + n]
                    j0 = 15 if side == 0 else 0
                    for r in range(8):
                        src = ghs[:, :, r, j0:j0 + 15]
                        sc_ = self.sel[:, side * 8 + r:side * 8 + r + 1]
                        if r == 0:
                            self.dve(V("tensor_scalar_mul", out=dst, in0=src,
                                                                                               scalar1=sc_),
                                     ["ghs", "const"], ["s_halo%d" % side])
                        else:
                            self.dve(V("scalar_tensor_tensor",
                                out=dst, in0=src, scalar=sc_, in1=dst, op0=ALU.mult, op1=ALU.add),
                                ["ghs", "const", "s_halo%d" % side], ["s_halo%d" % side])
            else:
                self.dve(V("memset", s_ext[:, :, 0:15], 0.0), [], ["s_ext", "s_halo0"])
                self.dve(V("memset", s_ext[:, :, 15 + n:30 + n], 0.0), [], ["s_ext", "s_halo1"])
            self.dma(s_ext[:, :, 15:15 + n], Sc["sT"][:, t0:t0 + n].rearrange("(c p) t -> p c t", p=128), ["sT"],
                     ["s_ext"], "s_ext")
            for c in range(4):
                eng = "dve"
                uk = "u%d" % c
                self.dve(V("tensor_scalar", out=u[:, c, 0:n], in0=s_ext[:, c, 0:n],
                                                              scalar1=cw[:, c, 0:1], scalar2=cv[:, 0, c:c + 1],
                                                              op0=ALU.mult, op1=ALU.add),
                         ["s_ext", "s_halo0", "s_halo1", "cw", "cv"], [uk], eng=eng)
                for k in range(1, 31):
                    self.dve(V("scalar_tensor_tensor", out=u[:, c, 0:n], in0=s_ext[:, c, k:k + n],
                                                                               scalar=cw[:, c, k:k + 1], in1=u[:, c, 0:n],
                                                                               op0=ALU.mult, op1=ALU.add),
                             ["s_ext", "s_halo0", "s_halo1", "cw", uk], [uk], eng=eng)
            for g0 in range(0, n, 512):
                gn = min(512, n - g0)
                b1 = self.bank()
                b2 = self.bank()
                for c in range(4):
                    self.mm(b1, self.ps(b1, gn), self.ones_f, u[:, c, g0:g0 + gn], c == 0, c == 3, ["u%d" % c, "const3"])
                for c in range(4):
                    self.act(sq[:, 0:gn], u[:, c, g0:g0 + gn], AF.Square, ["u%d" % c], ["sq"])
                    self.mm(b2, self.ps(b2, gn), self.ones_f, sq[:, 0:gn], c == 0, c == 3, ["sq", "const3"])
                self.act(mean[:, 0:gn], self.ps(b1, gn), AF.Copy, [], ["ps%d" % b1, "mean"], scale=1.0 / 512)
                self.dve(V("tensor_tensor", out=msq[:, 0:gn], in0=mean[:, 0:gn], in1=mean[:, 0:gn],
                                                           op=ALU.mult), ["mean"], ["msq"])
                self.dve(V("scalar_tensor_tensor", out=msq[:, 0:gn], in0=self.ps(b2, gn),
                                                                         scalar=1.0 / 512, in1=msq[:, 0:gn],
                                                                         op0=ALU.mult, op1=ALU.subtract),
                         ["msq"], ["ps%d" % b2, "msq"])
                self.act(rstd[:, 0:gn], msq[:, 0:gn], AF.Sqrt, ["msq"], ["rstd"], bias=self.epsc[:, 0:1])
                self.dve(V("reciprocal", out=rstd[:, 0:gn], in_=rstd[:, 0:gn]), ["rstd"], ["rstd"])
                for c in range(4):
                    self.dve(V("tensor_tensor", out=tt_[:, 0:gn], in0=u[:, c, g0:g0 + gn],
                                                                           in1=mean[:, 0:gn], op=ALU.subtract),
                             ["u%d" % c, "mean"], ["tt_"])
                    self.dve(V("tensor_tensor", out=tt_[:, 0:gn], in0=tt_[:, 0:gn], in1=rstd[:, 0:gn],
                                                               op=ALU.mult), ["tt_", "rstd"], ["tt_"])
                    st = yst[yi % 2]
                    sk = "yst%d" % (yi % 2)
                    yi += 1
                    self.act(st[:, 0:gn], tt_[:, 0:gn], AF.Silu, ["tt_", "cv"], [sk], scale=cv[:, 1, c:c + 1],
                             bias=cv[:, 2, c:c + 1])
                    self.dma(Sc["brT"][c * 128:(c + 1) * 128, t0 + g0:t0 + g0 + gn], st[:, 0:gn], [sk], ["brT"], sk)

    def ph_merge(self, l, ngroups):
        A, S, I, Sc = self.A, self.S, self.I, self.Sc
        groups = GROUPS[:ngroups]
        ncols = T if ngroups == 3 else TL
        hT_sb = self.load_hT(ncols)
        br = A.tile(16 * ncols, BF16).rearrange("p (k t) -> p k t", k=16)
        self.dma(br, Sc["brT"][:, 0:ncols].rearrange("(k p) t -> p k t", p=128), ["brT"], ["br"], "br")
        wg = [A.tile(4 * 16 * 128, BF16).rearrange("p (n k c) -> p n k c", n=4, k=16) for _ in range(2)]
        wb = [A.tile(4 * 4 * 128, BF16).rearrange("p (n k c) -> p n k c", n=4, k=4) for _ in range(2)]
        sg = [A.tile(512, F32) for _ in range(2)]
        m_ = A.tile(512, F32)
        t_ = A.tile(512, F32)
        mst = [A.tile(512, BF16) for _ in range(2)]
        si = 0
        mi = 0
        c_g = N_KV + 2560
        for j in range(16):
            i = j % 2
            wk = "wg%d" % i
            for n in range(4):
                src = I["w_in"][l][:, c_g + n * D + j * 128:c_g + n * D + (j + 1) * 128]
                self.dma(wg[i][:, n, :, :], src.rearrange("(k p) c -> p k c", p=128), ["full_w_in%d" % l], [wk], wk)
                srcb = I["w_branch"][l, n][:, j * 128:(j + 1) * 128]
                self.dma(wb[i][:, n, :, :], srcb.rearrange("(k p) c -> p k c", p=128), ["full_w_branch%d" % l], [wk], wk)
            for (g0, gn) in groups:
                for n in range(4):
                    bg = self.bank()
                    for k in range(16):
                        self.mm(bg, self.ps(bg, gn), wg[i][:, n, k, :], hT_sb[:, k, g0:g0 + gn], k == 0, k == 15,
                                [wk, "hT_sb"])
                    bp = self.bank()
                    for k in range(4):
                        self.mm(bp, self.ps(bp, gn), wb[i][:, n, k, :], br[:, n * 4 + k, g0:g0 + gn], k == 0, k == 3,
                                [wk, "br"])
                    s_ = sg[si % 2][:, 0:gn]
                    sk = "sg%d" % (si % 2)
                    si += 1
                    self.act(s_, self.ps(bg, gn), AF.Sigmoid, [], ["ps%d" % bg, sk])
                    if n == 0:
                        self.dve(V("tensor_tensor", out=m_[:, 0:gn], in0=self.ps(bp, gn),
                                                                                 in1=s_, op=ALU.mult),
                                 [sk], ["ps%d" % bp, "m_"])
                    else:
                        self.dve(V("tensor_tensor", out=t_[:, 0:gn], in0=self.ps(bp, gn),
                                                                                 in1=s_, op=ALU.mult),
                                 [sk], ["ps%d" % bp, "t_"])
                        self.dve(V("tensor_tensor", out=m_[:, 0:gn], in0=m_[:, 0:gn], in1=t_[:, 0:gn],
                                                                   op=ALU.add), ["t_", "m_"], ["m_"])
                st = mst[mi % 2]
                stk = "mst%d" % (mi % 2)
                mi += 1
                self.act(st[:, 0:gn], m_[:, 0:gn], AF.Copy, ["m_"], [stk])
                self.dma(Sc["mT"][j * 128:(j + 1) * 128, g0:g0 + gn], st[:, 0:gn], [stk], ["mT"], stk)

    def resid_update(self, l, goff, tiles, lhs_fn, nk, rhs_fn, rhs_keys_fn, lhs_key, cg_outer=None):
        pass

    def ph_out(self, l, ntiles):
        A, S, I, Sc = self.A, self.S, self.I, self.Sc
        wo = A.tile(16 * D, BF16).rearrange("p (k n) -> p k n", k=16)
        for q in range(4):
            self.wload(wo[:, :, q * 512:(q + 1) * 512], I["w_out"][l][:, q * 512:(q + 1) * 512], "wo", 16,
                       rkey="full_w_out%d" % l)
        G = []
        for r in range(2 if ntiles > 8 else 1):
            g_ = A.tile(D, F32)
            self.bcast_load(g_, Sc["modv"][l, r, 2 * D:3 * D], "G%d" % r)
            G.append(g_)
        xt = [A.tile(D, F32) for _ in range(2)]
        mt = [A.tile(16 * 128, BF16).rearrange("p (k t) -> p k t", k=16) for _ in range(2)]
        t_ = A.tile(512, F32)
        for tt in range(ntiles):
            i = tt % 2
            r = 0 if tt < 8 else 1
            xk, mk = "xt%d" % i, "mt%d" % i
            self.dma(xt[i], Sc["xres"][tt * 128:(tt + 1) * 128, :], ["xres"], [xk], xk)
            self.dma(mt[i], Sc["mT"][:, tt * 128:(tt + 1) * 128].rearrange("(k p) t -> p k t", p=128), ["mT"], [mk], mk)
            for cg in range(4):
                b = self.bank()
                for k in range(16):
                    self.mm(b, self.ps(b, 512), mt[i][:, k, :], wo[:, k, cg * 512:(cg + 1) * 512], k == 0, k == 15,
                            [mk, "wo"])
                self.dve(V("tensor_tensor", out=t_, in0=self.ps(b, 512),
                                                                    in1=G[r][:, cg * 512:(cg + 1) * 512], op=ALU.mult),
                         ["G%d" % r], ["ps%d" % b, "t_"])
                self.dve(V("tensor_tensor", out=xt[i][:, cg * 512:(cg + 1) * 512],
                                                                in0=xt[i][:, cg * 512:(cg + 1) * 512], in1=t_,
                                                                op=ALU.add), ["t_", xk], [xk])
            self.dma(Sc["xres"][tt * 128:(tt + 1) * 128, :], xt[i], [xk], ["xres"], xk)

    def ph_ffn(self, l, with_ctx):
        A, S, I, Sc = self.A, self.S, self.I, self.Sc
        NE = TL + 2
        h2 = A.tile(16 * NE, BF16).rearrange("p (k t) -> p k t", k=16)
        self.dma(h2[:, :, 1:1 + TL], Sc["hT"][:, :, 0:TL].rearrange("k p t -> p k t"), ["hT"], ["h2"], "h2")
        g2s = A.tile(16 * 16, BF16).rearrange("p (k r j) -> p k r j", k=16, r=8)
        for k in range(16):
            self.dma(g2s[:, k, :, :], Sc["gh2"].rearrange("(r k p) j -> k p r j", k=16, p=128)[k], ["gh2"], ["g2s"], "g2s")
        hacc = A.tile(32, F32).rearrange("p (s k) -> p s k", s=2)
        for side in range(2):
            j = 1 if side == 0 else 0
            for r in range(8):
                sc_ = self.sel[:, side * 8 + r:side * 8 + r + 1]
                src = g2s[:, :, r, j]
                dst = hacc[:, side, :]
                if r == 0:
                    self.dve(V("tensor_scalar_mul", out=dst, in0=src, scalar1=sc_),
                             ["g2s", "const"], ["hacc%d" % side])
                else:
                    self.dve(V("scalar_tensor_tensor",
                        out=dst, in0=src, scalar=sc_, in1=dst, op0=ALU.mult, op1=ALU.add),
                        ["g2s", "const", "hacc%d" % side], ["hacc%d" % side])
            col = 0 if side == 0 else NE - 1
            self.dve(V("tensor_copy", out=h2[:, :, col], in_=hacc[:, side, :]),
                     ["hacc%d" % side], ["h2e%d" % side])
        h2k = ["h2", "h2e0", "h2e1"]
        if with_ctx:
            NC_ = TC + 2
            h2c = A.tile(16 * NC_, BF16).rearrange("p (k t) -> p k t", k=16)
            self.dve(V("memset", h2c[:, :, 0:1], 0.0), [], ["h2c0"])
            self.dve(V("memset", h2c[:, :, NC_ - 1:NC_], 0.0), [], ["h2c1"])
            self.dma(h2c[:, :, 1:1 + TC], Sc["hT"][:, :, TL:T].rearrange("k p t -> p k t"), ["hT"], ["h2c"], "h2c")
            h2ck = ["h2c", "h2c0", "h2c1"]
        fw = A.tile(88 * 3, F32).rearrange("p (t k) -> p t k", k=3)
        fb = A.tile(88, F32)
        for k in range(3):
            self.load_T(fw[:, :, k], I["ffn_dw_w"][l, k].rearrange("(t p) -> t p", p=128), 88, None, "fw")
        self.load_T(fb, I["ffn_dw_b"][l].rearrange("(t p) -> t p", p=128), 88, None, "fb")
        wbuf = [[A.tile(16 * 512, BF16).rearrange("p (k n) -> p k n", k=16) for _ in range(2)] for _ in range(2)]
        ue = [A.tile(NE, F32) for _ in range(2)]
        y = [A.tile(TL, F32) for _ in range(2)]
        ast = [A.tile(TL, BF16) for _ in range(2)]
        ai = 0
        segs = [(0, TL, h2, h2k)] + ([(TL, TC, h2c, h2ck)] if with_ctx else [])
        for it in range(44):
            if it % 4 == 0:
                wi = (it // 4) % 2
                wk = "fw%d" % wi
                for ab in range(2):
                    c0 = ab * DFF + it * 128
                    self.wload(wbuf[wi][ab], I["ffn_w_up"][l][:, c0:c0 + 512], wk, 16, rkey="full_ffn_w_up%d" % l)
            co = (it % 4) * 128
            for (t0, n, hsrc, hkeys) in segs:
                ne = n + 2
                for ab in range(2):
                    ti = it + ab * 44
                    u_ = ue[ab]
                    uk = "ue%d" % ab
                    for c0 in range(0, ne, 342):
                        cn = min(342, ne - c0)
                        b = self.bank()
                        for k in range(16):
                            self.mm(b, self.ps(b, cn), wbuf[wi][ab][:, k, co:co + 128], hsrc[:, k, c0:c0 + cn],
                                    k == 0, k == 15, [wk] + hkeys)
                        self.act(u_[:, c0:c0 + cn], self.ps(b, cn), AF.Copy, [], ["ps%d" % b, uk])
                    y_ = y[ab][:, 0:n]
                    yk = "y%d" % ab
                    self.dve(V("tensor_scalar", out=y_, in0=u_[:, 0:n],
                                                                                 scalar1=fw[:, ti, 0:1],
                                                                                 scalar2=fb[:, ti:ti + 1],
                                                                                 op0=ALU.mult, op1=ALU.add),
                             [uk, "fw", "fb"], [yk])
                    for k in (1, 2):
                        self.dve(V("scalar_tensor_tensor",
                            out=y_, in0=u_[:, k:k + n], scalar=fw[:, ti, k:k + 1], in1=y_, op0=ALU.mult, op1=ALU.add),
                            [uk, "fw", yk], [yk])
                self.act(y[0][:, 0:n], y[0][:, 0:n], AF.Silu, ["y0"], ["y0"])
                st = ast[ai % 2]
                sk = "ast%d" % (ai % 2)
                ai += 1
                self.dve(V("tensor_tensor", out=st[:, 0:n], in0=y[0][:, 0:n], in1=y[1][:, 0:n],
                                                                op=ALU.mult), ["y0", "y1"], [sk])
                self.dma(Sc["actT"][it * 128:(it + 1) * 128, t0:t0 + n], st[:, 0:n], [sk], ["actT"], sk)

    def ph_down(self, l, with_ctx):
        A, S, I, Sc = self.A, self.S, self.I, self.Sc
        G = []
        for r in range(2 if with_ctx else 1):
            g_ = A.tile(D, F32)
            self.bcast_load(g_, Sc["modv"][l, r, 5 * D:6 * D], "G%d" % r)
            G.append(g_)
        at = A.tile(44 * 512, BF16).rearrange("p (k t) -> p k t", k=44)
        CW = 512
        wd = [A.tile(44 * CW, BF16).rearrange("p (k n) -> p k n", k=44) for _ in range(2)]
        xt = A.tile(4 * D, F32).rearrange("p (a d) -> p a d", a=4)
        t_ = A.tile(CW, F32)
        tgs = [(0, 512, 0), (512, 512, 0)] + ([(TL, TC, 1)] if with_ctx else [])
        wi = 0
        for (t0, n, r) in tgs:
            ntl = n // 128
            self.dma(at[:, :, 0:n], Sc["actT"][:, t0:t0 + n].rearrange("(k p) t -> p k t", p=128), ["actT"], ["at"], "at")
            for a in range(ntl):
                self.dma(xt[:, a, :], Sc["xres"][t0 + a * 128:t0 + (a + 1) * 128, :], ["xres"], ["xt"], "xt")
            for cg in range(D // CW):
                w = wd[wi % 2]
                wk = "wd%d" % (wi % 2)
                wi += 1
                self.wload(w, I["ffn_w_down"][l][:, cg * CW:(cg + 1) * CW], wk, 44, split=11,
                           rkey="full_ffn_w_down%d" % l)
                for a in range(ntl):
                    b = self.bank()
                    for k in range(44):
                        self.mm(b, self.ps(b, CW), at[:, k, a * 128:(a + 1) * 128], w[:, k, :], k == 0, k == 43,
                                ["at", wk])
                    self.dve(V("tensor_tensor", out=t_, in0=self.ps(b, CW),
                                                                        in1=G[r][:, cg * CW:(cg + 1) * CW],
                                                                        op=ALU.mult),
                             ["G%d" % r], ["ps%d" % b, "t_"])
                    self.dve(V("tensor_tensor", out=xt[:, a, cg * CW:(cg + 1) * CW],
                                                                    in0=xt[:, a, cg * CW:(cg + 1) * CW], in1=t_,
                                                                    op=ALU.add), ["t_", "xt"], ["xt"])
            for a in range(ntl):
                self.dma(Sc["xres"][t0 + a * 128:t0 + (a + 1) * 128, :], xt[:, a, :], ["xt"], ["xres"], "xt")

    def ph_final(self, out):
        self.ph_norm(0, 1, 8, final_out=out)


def _rope_np(row, col, rot_dim):
    axis_dim = rot_dim // 2
    inv = np.power(np.float32(10000.0), -np.arange(0, axis_dim, 2, dtype=np.float32) / np.float32(axis_dim)).astype(np.float32)
    ang = np.concatenate([row.astype(np.float32)[:, None] * inv, col.astype(np.float32)[:, None] * inv], axis=-1)
    return np.cos(ang).astype(np.float32), np.sin(ang).astype(np.float32)


def _consts(core):
    t = np.arange(core * TL, (core + 1) * TL)
    row, col = t // 64, t % 64
    cg, sg = _rope_np(row, col, 128)
    c6, s6 = _rope_np(row, col, 64)
    ropeg_c = np.ones((128, T), np.float32)
    ropeg_s = np.zeros((128, T), np.float32)
    ropeg_c[0:64, 0:TL] = cg.T
    ropeg_c[64:128, 0:TL] = cg.T
    ropeg_s[0:64, 0:TL] = -sg.T
    ropeg_s[64:128, 0:TL] = sg.T
    rope6_c = np.ones((128, T), np.float32)
    rope6_s = np.zeros((128, T), np.float32)
    for blk in range(2):
        o = blk * 64
        rope6_c[o:o + 32, 0:TL] = c6.T
        rope6_c[o + 32:o + 64, 0:TL] = c6.T
        rope6_s[o:o + 32, 0:TL] = -s6.T
        rope6_s[o + 32:o + 64, 0:TL] = s6.T
    sel = np.zeros((128, 16), np.float32)
    if core > 0:
        sel[:, core - 1] = 1.0
    if core < NCORE - 1:
        sel[:, 8 + core + 1] = 1.0
    ident = np.eye(128, dtype=np.float32)
    perm64 = np.zeros((128, 128), np.float32)
    perm32 = np.zeros((128, 128), np.float32)
    for i in range(128):
        perm64[i, (i + 64) % 128] = 1.0
        perm32[i, i ^ 32] = 1.0
    return dict(ropeg_c=ropeg_c, ropeg_s=ropeg_s, rope6_c=rope6_c, rope6_s=rope6_s, sel=sel, ident_f=ident,
                perm64=perm64, perm32=perm32)


WEIGHT_NAMES = ["w_mod", "b_mod", "norm1_g", "norm2_g", "w_in", "conv_w", "conv_b", "conv_ln_g", "conv_ln_b",
                "gqa_q_g", "gqa_k_g", "mla_q_g", "mla_w_q_up", "mla_kv_g", "mla_w_kv_up", "diff_lq1", "diff_lk1",
                "diff_lq2", "diff_lk2", "diff_g", "w_branch", "w_out", "ffn_w_up", "ffn_dw_w", "ffn_dw_b",
                "ffn_w_down", "final_g"]


def make_in_maps(inputs):
    x = np.ascontiguousarray(np.asarray(inputs["x"], dtype=np.float32)[0])
    ctx = np.ascontiguousarray(np.asarray(inputs["ctx"], dtype=np.float32)[0])
    cc = np.ascontiguousarray(np.stack([np.asarray(inputs["c"], np.float32)[0], np.asarray(inputs["c_ctx"], np.float32)]))
    shared = {nm: np.ascontiguousarray(np.asarray(inputs[nm], dtype=np.float32)) for nm in WEIGHT_NAMES}
    big = {nm: (R, C) for nm, R, C in BIGW}
    maps = []
    for r in range(NCORE):
        m = {}
        for nm, a in shared.items():
            if nm in big:
                R, C = big[nm]
                a2 = a.reshape(DEPTH, R, C)
                m[nm] = np.ascontiguousarray(a2[:, r * (R // NCORE):(r + 1) * (R // NCORE), :])
            elif nm == "w_mod":
                m[nm] = np.ascontiguousarray(a[:, :, r * MODC:(r + 1) * MODC])
            elif nm == "b_mod":
                m[nm] = np.ascontiguousarray(a[:, r * MODC:(r + 1) * MODC])
            else:
                m[nm] = a
        m["x"] = np.ascontiguousarray(x[r * TL:(r + 1) * TL])
        m["ctx"] = ctx
        m["cc"] = cc
        m.update(_consts(r))
        maps.append(m)
    return maps


def kernel(**inputs):
    nc = Builder().build()
    maps = make_in_maps(inputs)
    res = run_bass_kernel_spmd(nc, maps, core_ids=list(range(NCORE)))
    out = np.concatenate([np.asarray(res.results[r]["out"], dtype=np.float32) for r in range(NCORE)], axis=0)
    return out[None]
```

```python
import math
import numpy as np
import ml_dtypes
import concourse.bass as bass
import concourse.mybir as mybir
from concourse.bass_utils import run_bass_kernel_spmd
from contextlib import ExitStack

F32 = mybir.dt.float32
BF16 = mybir.dt.bfloat16
AF = mybir.ActivationFunctionType
ALU = mybir.AluOpType
AX = mybir.AxisListType

D = 2048
SEQ = 8192
NCORE = 8
TL = 1024
TC = 256
T = TL + TC
DEPTH = 2
N_KV = 1856
N_IN = 12608
DFF = 5632
EPS = 1e-6
KROWS = 1344
VCOLS = 1280
QROWS = 1792
GROUPS = [(0, 512), (512, 512), (1024, 256)]
MODC = 6 * D // NCORE
BIGW = [("w_in", D, N_IN), ("w_branch", D, D), ("w_out", D, D), ("ffn_w_up", D, 2 * DFF), ("ffn_w_down", DFF, D)]


def V(name, *a, **kw):
    return lambda e: getattr(e, name)(*a, **kw)


class _Op:
    __slots__ = ("eng", "fn", "deps", "signals", "dma_key", "clk", "val", "vc", "waits", "inc", "ep", "det")

    def __init__(self, eng, fn, dma_key, inc):
        self.eng = eng
        self.fn = fn
        self.deps = []
        self.signals = False
        self.dma_key = dma_key
        self.clk = None
        self.val = 0
        self.vc = None
        self.waits = []
        self.inc = inc


class Sched:
    def __init__(self, nc, sync_same=True):
        self.nc = nc
        self.ops = []
        self.last_writer = {}
        self.readers = {}
        self.sync_same = sync_same
        self.epoch = 0

    def op(self, eng, fn, reads=(), writes=(), dma_key=None, inc=16, detached=False):
        o = _Op(eng, fn, dma_key, inc)
        o.ep = self.epoch
        o.det = detached
        idx = len(self.ops)
        deps = set()
        for k in reads:
            w = self.last_writer.get(k)
            if w is not None:
                deps.add(w)
        for k in writes:
            w = self.last_writer.get(k)
            if w is not None:
                deps.add(w)
            for r in self.readers.get(k, ()):
                deps.add(r)
        for d in deps:
            p = self.ops[d]
            if p.dma_key is None and p.eng == eng:
                if eng == "pe" or eng == "sp":
                    continue
                if not self.sync_same:
                    continue
                raw = any(self.last_writer.get(k) == d for k in reads) or any(
                    self.last_writer.get(k) == d for k in writes)
                if not raw:
                    continue
            o.deps.append(d)
            p.signals = True
        for k in reads:
            self.readers.setdefault(k, []).append(idx)
        for k in writes:
            self.last_writer[k] = idx
            self.readers[k] = []
        if dma_key is not None:
            o.signals = True
        self.ops.append(o)
        return idx

    def barrier(self):
        seen = set()
        for p in reversed(self.ops):
            if p.eng == "*barrier*":
                break
            if p.eng not in seen:
                seen.add(p.eng)
                p.signals = True
        bo = _Op("*barrier*", None, None, 0)
        bo.det = False
        self.ops.append(bo)
        self.last_writer = {k: w for k, w in self.last_writer.items() if self.ops[w].det}
        self.readers = {}

    def finalize_and_emit(self):
        nc = self.nc
        counts = {}
        known = {}
        ENGS = ("pe", "act", "dve", "pool", "sp")
        det_clks = set(("dma", o.dma_key) for o in self.ops if o.eng != "*barrier*" and o.det)
        for o in self.ops:
            E = o.eng
            if E == "*barrier*":
                bw = {}
                for EE in ENGS:
                    kn = known.setdefault(EE, {})
                    bw[EE] = [(c, v) for c, v in counts.items() if kn.get(c, 0) < v and c not in det_clks]
                    for c, v in counts.items():
                        if c not in det_clks:
                            kn[c] = max(kn.get(c, 0), v)
                o.waits = bw
                continue
            kn = known.setdefault(E, {})
            for d in sorted(o.deps):
                p = self.ops[d]
                if kn.get(p.clk, 0) < p.val:
                    o.waits.append((p.clk, p.val))
                    for c, v in p.vc.items():
                        if kn.get(c, 0) < v:
                            kn[c] = v
                    kn[p.clk] = p.val
            best = {}
            for c, v in o.waits:
                best[c] = max(best.get(c, 0), v)
            o.waits = list(best.items())
            if o.signals:
                if o.dma_key is not None:
                    o.clk = ("dma", o.dma_key)
                    counts[o.clk] = counts.get(o.clk, 0) + o.inc
                else:
                    o.clk = (E, o.ep)
                    counts[o.clk] = counts.get(o.clk, 0) + 1
                o.val = counts[o.clk]
                o.vc = dict(kn)
        clks = list(counts.keys())
        self.n_sems = len(clks)
        self.max_count = max(counts.values()) if counts else 0
        with ExitStack() as es:
            sems = {}
            for i, c in enumerate(clks):
                sems[c] = es.enter_context(nc.semaphore("s%d" % i))
            block = es.enter_context(nc.Block())
            per_eng = {}
            for o in self.ops:
                if o.eng == "*barrier*":
                    for EE in ENGS:
                        per_eng.setdefault(EE, []).append(o)
                else:
                    per_eng.setdefault(o.eng, []).append(o)
            final = dict(counts)

            def emit(engh, lst, E):
                for o in lst:
                    if o.eng == "*barrier*":
                        for c, v in o.waits[E]:
                            engh.wait_ge(sems[c], v)
                        continue
                    for c, v in o.waits:
                        engh.wait_ge(sems[c], v)
                    ins = o.fn(engh)
                    if o.signals:
                        ins.then_inc(sems[o.clk], o.inc if o.dma_key is not None else 1)
                if E == "sp":
                    for c, v in final.items():
                        engh.wait_ge(sems[c], v)

            names = {"pe": "tensor", "act": "scalar", "dve": "vector", "pool": "gpsimd", "sp": "sync"}
            for E in ENGS:
                lst = per_eng.get(E, [])

                def mk(lst=lst, E=E):
                    def f(engh):
                        emit(engh, lst, E)
                    return f
                getattr(block, names[E])(mk())


class Arena:
    def __init__(self, nc, es, nbytes):
        self.t = es.enter_context(nc.sbuf_tensor("arena", [128, nbytes // 2], BF16))
        self.nbytes = nbytes
        self.off = 0
        self.base = 0
        self.n = 0

    def reset(self):
        self.off = self.base

    def persist(self):
        self.base = self.off

    def tile(self, n, dt, parts=128):
        sz = 4 if dt == F32 else 2
        nb = (n * sz + 63) // 64 * 64
        assert self.off + nb <= self.nbytes, ("arena overflow", self.off, nb, self.nbytes)
        a = self.t[0:parts, self.off // 2:(self.off + nb) // 2]
        self.off += nb
        if dt == F32:
            a = a.bitcast(F32)
        self.n += 1
        return a[:, 0:n]


class Builder:
    def __init__(self, debug=(), stop_after=None):
        self.debug = set(debug)
        self.stop_after = stop_after
        self.nc = bass.Bass("TRN2", target_bir_lowering=False)
        self.uid = 0

    def key(self, base):
        self.uid += 1
        return "%s#%d" % (base, self.uid)

    def din(self, name, shape, dt=F32):
        return self.nc.dram_tensor(name, list(shape), dt, kind="ExternalInput").ap()

    def dscr(self, name, shape, dt):
        kind = "ExternalOutput" if name in self.debug else "Internal"
        return self.nc.dram_tensor(name, list(shape), dt, kind=kind).ap()

    def dma(self, out, in_, reads, writes, key, eng="sp", **kw):
        self.S.op(eng, V("dma_start", out=out, in_=in_, **kw), reads=reads, writes=writes, dma_key=key)

    def bank(self):
        b = self.bank_i
        self.bank_i = (self.bank_i + 1) % 8
        return b

    def mm(self, b, out, lhsT, rhs, start, stop, reads):
        self.S.op("pe", V("matmul", out, lhsT=lhsT, rhs=rhs, start=start, stop=stop),
                  reads=reads, writes=["ps%d" % b])

    def act(self, out, in_, func, reads, writes, **kw):
        self.S.op("act", V("activation", out=out, in_=in_, func=func, **kw), reads=reads, writes=writes)

    def dve(self, fn, reads, writes, eng="dve"):
        self.S.op(eng, fn, reads=reads, writes=writes)

    def ps(self, b, n, parts=128, dt=F32):
        p = self.psum[b]
        if dt == BF16:
            return p[0:parts, :].bitcast(BF16)[:, 0:n]
        return p[0:parts, 0:n]

    def wload(self, dst3, src2, key, nk, split=4, cast=False, rkey=None):
        step = max(1, nk // split)
        for k0 in range(0, nk, step):
            k1 = min(nk, k0 + step)
            src = src2[k0 * 128:k1 * 128, :].rearrange("(k p) n -> p k n", p=128)
            self.dma(dst3[:, k0:k1, :], src, [rkey] if rkey else [], [key], key, eng="pool" if cast else "sp")

    def load_T(self, dst, src_rows, n, rkey, wkey):
        tmp = self.tmpT
        k = self.key("tmpT")
        self.dma(tmp[0:n, :], src_rows, [], ["tmpT"], "tmpT")
        b = self.bank()
        self.S.op("pe", V("transpose", self.ps(b, n), tmp[0:n, :], self.ident_f[0:n, 0:n]),
                  reads=["tmpT", "const"], writes=["ps%d" % b])
        self.act(dst, self.ps(b, n), AF.Copy, [], ["ps%d" % b, wkey])

    def build(self):
        nc = self.nc
        B = self
        I = {}
        I["x"] = self.din("x", [TL, D])
        I["ctx"] = self.din("ctx", [TC, D])
        I["cc"] = self.din("cc", [2, D])
        for nm, shp in [("w_mod", [DEPTH, D, MODC]), ("b_mod", [DEPTH, MODC]), ("norm1_g", [DEPTH, D]),
                        ("norm2_g", [DEPTH, D]), ("w_in", [DEPTH, D // NCORE, N_IN]), ("conv_w", [DEPTH, 31, 512]),
                        ("conv_b", [DEPTH, 512]), ("conv_ln_g", [DEPTH, 512]), ("conv_ln_b", [DEPTH, 512]),
                        ("gqa_q_g", [DEPTH, 128]), ("gqa_k_g", [DEPTH, 128]), ("mla_q_g", [DEPTH, 512]),
                        ("mla_w_q_up", [DEPTH, 512, 768]), ("mla_kv_g", [DEPTH, 256]),
                        ("mla_w_kv_up", [DEPTH, 256, 1024]), ("diff_lq1", [DEPTH, 64]), ("diff_lk1", [DEPTH, 64]),
                        ("diff_lq2", [DEPTH, 64]), ("diff_lk2", [DEPTH, 64]), ("diff_g", [DEPTH, 128]),
                        ("w_branch", [DEPTH, D // NCORE, D]), ("w_out", [DEPTH, D // NCORE, D]),
                        ("ffn_w_up", [DEPTH, D // NCORE, 2 * DFF]),
                        ("ffn_dw_w", [DEPTH, 3, 2 * DFF]), ("ffn_dw_b", [DEPTH, 2 * DFF]),
                        ("ffn_w_down", [DEPTH, DFF // NCORE, D]), ("final_g", [D])]:
            I[nm] = self.din(nm, shp)
        I["ropeg_c"] = self.din("ropeg_c", [128, T])
        I["ropeg_s"] = self.din("ropeg_s", [128, T])
        I["rope6_c"] = self.din("rope6_c", [128, T])
        I["rope6_s"] = self.din("rope6_s", [128, T])
        I["sel"] = self.din("sel", [128, 16])
        I["ident_f"] = self.din("ident_f", [128, 128])
        I["perm64"] = self.din("perm64", [128, 128])
        I["perm32"] = self.din("perm32", [128, 128])
        self.I = I
        out = nc.dram_tensor("out", [TL, D], F32, kind="ExternalOutput").ap()

        Sc = {}
        for nm, R, C in BIGW:
            Sc["sh_" + nm] = self.dscr("sh_" + nm, [DEPTH, R // NCORE, C], BF16)
            Sc["full_" + nm] = self.dscr("full_" + nm, [DEPTH, R, C], BF16)
        Sc["modp"] = self.dscr("modp", [DEPTH * 2, MODC], F32)
        Sc["modg"] = self.dscr("modg", [NCORE * DEPTH * 2, MODC], F32)
        Sc["xres"] = self.dscr("xres", [T, D], F32)
        Sc["modv"] = self.dscr("modv", [DEPTH, 2, 6 * D], F32)
        Sc["hT"] = self.dscr("hT", [16, 128, T], BF16)
        Sc["pay_k"] = self.dscr("pay_k", [KROWS, TL], BF16)
        Sc["kc"] = self.dscr("kc", [KROWS, TC], BF16)
        Sc["pay_v"] = self.dscr("pay_v", [TL, VCOLS], BF16)
        Sc["vc"] = self.dscr("vc", [TC, VCOLS], BF16)
        Sc["gk"] = self.dscr("gk", [NCORE * KROWS, TL], BF16)
        Sc["gv"] = self.dscr("gv", [NCORE * TL, VCOLS], BF16)
        Sc["sT"] = self.dscr("sT", [512, T], F32)
        Sc["pay_h"] = self.dscr("pay_h", [512, 30], F32)
        Sc["gh"] = self.dscr("gh", [NCORE * 512, 30], F32)
        Sc["qT"] = self.dscr("qT", [QROWS, T], BF16)
        Sc["brT"] = self.dscr("brT", [D, T], BF16)
        Sc["mT"] = self.dscr("mT", [D, T], BF16)
        Sc["pay_h2"] = self.dscr("pay_h2", [D, 2], BF16)
        Sc["gh2"] = self.dscr("gh2", [NCORE * D, 2], BF16)
        Sc["actT"] = self.dscr("actT", [DFF, T], BF16)
        self.Sc = Sc

        with ExitStack() as es:
            es.enter_context(nc.allow_non_contiguous_dma(reason="small strided vectors / boundary columns"))
            self.A = A = Arena(nc, es, 207 * 1024)
            self.psum = [es.enter_context(nc.psum_tensor("psb%d" % i, [128, 512], F32)) for i in range(8)]
            self.bank_i = 0
            self.S = S = Sched(nc)
            self.ident_f = A.tile(128, F32)
            self.perm64 = A.tile(128, F32)
            self.perm32 = A.tile(128, F32)
            self.ident_b = A.tile(128, BF16)
            self.ones_f = A.tile(128, F32)
            self.ones_b = A.tile(128, BF16)
            self.ropeg_c = A.tile(T, F32)
            self.ropeg_s = A.tile(T, F32)
            self.rope6_c = A.tile(T, F32)
            self.rope6_s = A.tile(T, F32)
            self.sel = A.tile(16, F32)
            self.tmpT = A.tile(128, F32)
            self.epsc = A.tile(1, F32)
            A.persist()
            for dst, nm in [(self.ident_f, "ident_f"), (self.perm64, "perm64"), (self.perm32, "perm32"),
                            (self.ropeg_c, "ropeg_c"), (self.ropeg_s, "ropeg_s"), (self.rope6_c, "rope6_c"),
                            (self.rope6_s, "rope6_s"), (self.sel, "sel")]:
                self.dma(dst, I[nm], [], ["const"], "const")
            self.dve(V("tensor_copy", out=self.ident_b, in_=self.ident_f), ["const"], ["const2"])
            self.dve(V("memset", self.ones_f, 1.0), [], ["const3"])
            self.dve(V("memset", self.ones_b, 1.0), [], ["const4"])
            self.dve(V("memset", self.epsc, EPS), [], ["const5"])
            self.dma(Sc["xres"][0:TL, :], I["x"], [], ["xres"], "xres")
            self.dma(Sc["xres"][TL:T, :], I["ctx"], [], ["xres"], "xres")
            S.barrier()
            phases = []
            phases.append(("gather", lambda: self.ph_gather()))
            phases.append(("mod", lambda: self.ph_mod()))
            for l in range(DEPTH):
                last = l == DEPTH - 1
                phases.append(("norm1_%d" % l, lambda l=l: self.ph_norm(l, 1, 10)))
                phases.append(("kv_%d" % l, lambda l=l: self.ph_kv(l)))
                phases.append(("q_%d" % l, lambda l=l, last=last: self.ph_q(l, 2 if last else 3)))
                phases.append(("att_%d" % l, lambda l=l, last=last: self.ph_att(l, 2 if last else 3)))
                phases.append(("conv_%d" % l, lambda l=l, last=last: self.ph_conv(l, not last)))
                phases.append(("merge_%d" % l, lambda l=l, last=last: self.ph_merge(l, 2 if last else 3)))
                phases.append(("out_%d" % l, lambda l=l, last=last: self.ph_out(l, 8 if last else 10)))
                phases.append(("norm2_%d" % l, lambda l=l, last=last: self.ph_norm(l, 2, 8 if last else 10)))
                phases.append(("ffn_%d" % l, lambda l=l, last=last: self.ph_ffn(l, not last)))
                phases.append(("down_%d" % l, lambda l=l, last=last: self.ph_down(l, not last)))
            phases.append(("final", lambda: self.ph_final(out)))
            for nm, fn in phases:
                A.reset()
                if nm.startswith("norm1_") or nm == "final":
                    S.epoch += 1
                fn()
                S.barrier()
                if self.stop_after == nm:
                    break
            S.finalize_and_emit()
        return nc

    def ph_gather(self):
        I, Sc = self.I, self.Sc
        self.Iext = dict(I)
        for nm, R, C in BIGW:
            if nm == "w_branch":
                I[nm] = Sc["full_" + nm].rearrange("l (n c) d -> l n c d", n=4)
            else:
                I[nm] = Sc["full_" + nm]
        self.gather_w([("w_in", 0)])

    def gather_w(self, lst):
        Sc = self.Sc
        big = {nm: (R, C) for nm, R, C in BIGW}
        for nm, l in lst:
            R, C = big[nm]
            rs = R // NCORE
            k = "sh_%s%d" % (nm, l)
            nsp = 4
            step = rs // nsp
            for q in range(nsp):
                self.S.op("pool", V("dma_start", out=Sc["sh_" + nm][l, q * step:(q + 1) * step, :],
                                    in_=self.Iext[nm][l, q * step:(q + 1) * step, :]),
                          reads=[], writes=[k], dma_key="shc", detached=True)
            self.S.op("pool", V("collective_compute", "AllGather", ALU.bypass,
                                replica_groups=[list(range(NCORE))], ins=[Sc["sh_" + nm][l]],
                                outs=[Sc["full_" + nm][l]]),
                      reads=[k], writes=["full_%s%d" % (nm, l)], dma_key="ccw", inc=1, detached=True)

    def ph_mod(self):
        A, S, I, Sc = self.A, self.S, self.I, self.Sc
        ccT = A.tile(32, F32)
        scT = A.tile(32, BF16)
        for r in range(2):
            self.dma(ccT.rearrange("p (k r) -> p k r", r=2)[:, :, r], I["cc"][r].rearrange("(k p) -> p k", p=128),
                     [], ["ccT"], "ccT", allow_slow_non_contiguous=True)
        self.act(scT, ccT, AF.Silu, ["ccT"], ["scT"])
        sc3 = scT.rearrange("p (k r) -> p k r", r=2)
        modsb = A.tile(MODC, F32, parts=2)
        bsb = A.tile(MODC, F32, parts=2)
        wbuf = [A.tile(16 * 512, BF16).rearrange("p (k n) -> p k n", k=16) for _ in range(2)]
        wi = 0
        for l in range(DEPTH):
            for r in range(2):
                self.dma(bsb[r:r + 1, :], I["b_mod"][l:l + 1, :], [], ["bsb"], "bsb")
            for cg in range(MODC // 512):
                w = wbuf[wi % 2]
                wk = "modw%d" % (wi % 2)
                wi += 1
                self.wload(w, I["w_mod"][l][:, cg * 512:(cg + 1) * 512], wk, 16, cast=True)
                b = self.bank()
                for k in range(16):
                    self.mm(b, self.ps(b, 512, parts=2), sc3[:, k, :], w[:, k, :], k == 0, k == 15, ["scT", wk])
                self.dve(V("tensor_tensor", out=modsb[:, cg * 512:(cg + 1) * 512], in0=self.ps(b, 512, parts=2),
                           in1=bsb[:, cg * 512:(cg + 1) * 512], op=ALU.add),
                         ["bsb"], ["ps%d" % b, "modsb"])
            self.dma(Sc["modp"][2 * l:2 * l + 2, :], modsb, ["modsb"], ["modp"], "modp")
        self.S.op("pool", V("collective_compute", "AllGather", ALU.bypass, replica_groups=[list(range(NCORE))],
                            ins=[Sc["modp"]], outs=[Sc["modg"]]),
                  reads=["modp"], writes=["modg"], dma_key="cc", inc=1)
        mg = Sc["modg"].rearrange("(r q) c -> q r c", q=2 * DEPTH)
        for l in range(DEPTH):
            for s_ in range(2):
                self.dma(Sc["modv"][l, s_].rearrange("(r c) -> r c", c=MODC), mg[2 * l + s_], ["modg"], ["modv"], "modv")

    def bcast_load(self, dst, src_row, key):
        self.dma(dst, src_row.partition_broadcast(128), [], [key], key)

    def ph_norm(self, l, which, ntiles, final_out=None):
        A, S, I, Sc = self.A, self.S, self.I, self.Sc
        ncols = ntiles * 128
        hT_sb = A.tile(16 * ncols, BF16).rearrange("p (k t) -> p k t", k=16)
        AB = {}
        gn = A.tile(D, F32)
        if final_out is None:
            self.bcast_load(gn, (I["norm1_g"] if which == 1 else I["norm2_g"])[l], "gn")
            off_sh = 0 if which == 1 else 3 * D
            off_sc = off_sh + D
            for r in range(2 if ntiles > 8 else 1):
                a_t = A.tile(D, F32)
                b_t = A.tile(D, F32)
                self.bcast_load(a_t, Sc["modv"][l, r, off_sc:off_sc + D], "A%d" % r)
                self.bcast_load(b_t, Sc["modv"][l, r, off_sh:off_sh + D], "B%d" % r)
                self.dve(V("scalar_tensor_tensor", out=a_t, in0=a_t, scalar=1.0, in1=gn,
                                                                     op0=ALU.add, op1=ALU.mult),
                         ["gn", "A%d" % r], ["A%d" % r])
                AB[r] = (a_t, b_t)
        else:
            self.bcast_load(gn, I["final_g"], "gn")
        xt = [A.tile(D, F32) for _ in range(2)]
        hf = A.tile(D, F32)
        hb = [A.tile(D, BF16) for _ in range(2)]
        ss = A.tile(4, F32)
        for tt in range(ntiles):
            x_ = xt[tt % 2]
            xk = "xt%d" % (tt % 2)
            r = 0 if tt < 8 else 1
            self.dma(x_, Sc["xres"][tt * 128:(tt + 1) * 128, :], ["xres"], [xk], xk)
            self.dve(V("memset", ss[:, 0:1], 0.0), [], ["ss"])
            self.act(hf, x_, AF.Square, [xk], ["hf", "ss"], accum_out=ss[:, 0:1])
            self.act(ss[:, 1:2], ss[:, 0:1], AF.Sqrt, ["ss"], ["ss1"], scale=1.0 / D, bias=self.epsc[:, 0:1])
            self.dve(V("reciprocal", out=ss[:, 2:3], in_=ss[:, 1:2]), ["ss1"], ["ss2"])
            if final_out is not None:
                self.dve(V("scalar_tensor_tensor", out=hf, in0=x_, scalar=ss[:, 2:3], in1=gn,
                                                                   op0=ALU.mult, op1=ALU.mult),
                         [xk, "ss2", "gn"], ["hf"])
                self.dma(final_out[tt * 128:(tt + 1) * 128, :], hf, ["hf"], ["outf"], "outf")
                continue
            a_t, b_t = AB[r]
            h_ = hb[tt % 2]
            hk = "hb%d" % (tt % 2)
            self.dve(V("scalar_tensor_tensor", out=hf, in0=x_, scalar=ss[:, 2:3], in1=a_t,
                                                                        op0=ALU.mult, op1=ALU.mult),
                     [xk, "ss2", "A%d" % r], ["hf"])
            self.dve(V("tensor_tensor", out=h_, in0=hf, in1=b_t, op=ALU.add),
                     ["hf", "B%d" % r], [hk])
            for half in range(2):
                b = self.bank()
                pb = self.ps(b, 1024, dt=BF16)
                for j in range(8):
                    k = half * 8 + j
                    self.S.op("pe", V("transpose",
                        pb[:, j * 128:(j + 1) * 128], h_[:, k * 128:(k + 1) * 128], self.ident_b),
                        reads=[hk, "const2"], writes=["ps%d" % b])
                dst = hT_sb[:, half * 8:(half + 1) * 8, tt * 128:(tt + 1) * 128]
                src = pb.rearrange("p (j t) -> p j t", j=8)
                if half == 0:
                    self.act(dst, src, AF.Copy, [], ["ps%d" % b, "hT_sb"])
                else:
                    self.dve(V("tensor_copy", out=dst, in_=src), [], ["ps%d" % b, "hT_sb"])
        if final_out is not None:
            return
        self.dma(Sc["hT"][:, :, 0:ncols].rearrange("k p t -> p k t"), hT_sb, ["hT_sb"], ["hT"], "hT")
        if which == 2:
            self.dma(Sc["pay_h2"][:, 0:1].rearrange("(k p) o -> p k o", p=128), hT_sb[:, :, 0:1],
                     ["hT_sb"], ["pay_h2"], "pay_h2")
            self.dma(Sc["pay_h2"][:, 1:2].rearrange("(k p) o -> p k o", p=128), hT_sb[:, :, TL - 1:TL],
                     ["hT_sb"], ["pay_h2"], "pay_h2")
            self.S.op("pool", V("collective_compute", "AllGather", ALU.bypass,
                                                             replica_groups=[list(range(NCORE))],
                                                             ins=[Sc["pay_h2"]], outs=[Sc["gh2"]]),
                      reads=["pay_h2"], writes=["gh2"], dma_key="cc", inc=1)

    def rstd_from_sum(self, b, n, cnt, rkey):
        r = self.rs_t[self.rs_i % 2][:, 0:n]
        k = "rs%d" % (self.rs_i % 2)
        self.rs_i += 1
        self.act(r, self.ps(b, n), AF.Sqrt, [], ["ps%d" % b, k], scale=1.0 / cnt, bias=self.epsc[:, 0:1])
        self.dve(V("reciprocal", out=r, in_=r), [k], [k])
        return r, k

    def rope(self, xf, xk, n, c0, kind, parts=128):
        perm = self.perm64 if kind == "g" else self.perm32
        cs = (self.ropeg_c if kind == "g" else self.rope6_c)[0:parts, c0:c0 + n]
        sn = (self.ropeg_s if kind == "g" else self.rope6_s)[0:parts, c0:c0 + n]
        b = self.bank()
        self.mm(b, self.ps(b, n, parts=parts), perm[0:parts, 0:parts], xf, True, True, [xk, "const"])
        t2 = self.rp_t[0:parts, 0:n]
        self.dve(V("tensor_tensor", out=t2, in0=self.ps(b, n, parts=parts), in1=sn, op=ALU.mult),
                 ["const"], ["ps%d" % b, "rp_t"])
        self.dve(V("tensor_tensor", out=xf, in0=xf, in1=cs, op=ALU.mult), [xk, "const"], [xk])
        self.dve(V("tensor_tensor", out=xf, in0=xf, in1=t2, op=ALU.add), [xk, "rp_t"], [xk])

    def fm_tiles(self, w3, wk, hT_sb, col0, M, g0, gn):
        b = self.bank()
        for k in range(16):
            self.mm(b, self.ps(b, gn, parts=M), w3[:, k, col0:col0 + M], hT_sb[:, k, g0:g0 + gn],
                    k == 0, k == 15, [wk, "hT_sb"])
        return b

    def alloc_fm_scratch(self):
        A = self.A
        self.rs_t = [A.tile(512, F32) for _ in range(2)]
        self.rs_i = 0
        self.rp_t = A.tile(512, F32)
        self.xf_t = [A.tile(512, F32) for _ in range(6)]
        self.xf_i = 0
        self.sq_t = A.tile(512, F32)
        self.st_t = [A.tile(512, BF16) for _ in range(4)]
        self.st_i = 0

    def xf(self):
        i = self.xf_i % 6
        self.xf_i += 1
        return self.xf_t[i], "xf%d" % i

    def stg(self):
        i = self.st_i % 4
        self.st_i += 1
        return self.st_t[i], "st%d" % i

    def store_fm(self, src, sk, parts, n, g0, dst_lat, dst_ctx, row0, dkey):
        st, stk = self.stg()
        self.act(st[0:parts, 0:n], src, AF.Copy, [sk], [stk])
        if g0 < TL:
            self.dma(dst_lat[row0:row0 + parts, g0:g0 + n], st[0:parts, 0:n], [stk], [dkey], stk)
        else:
            self.dma(dst_ctx[row0:row0 + parts, g0 - TL:g0 - TL + n], st[0:parts, 0:n], [stk], [dkey], stk)

    def load_hT(self, ncols):
        hT_sb = self.A.tile(16 * ncols, BF16).rearrange("p (k t) -> p k t", k=16)
        self.dma(hT_sb, self.Sc["hT"][:, :, 0:ncols].rearrange("k p t -> p k t"), ["hT"], ["hT_sb"], "hT_sb")
        return hT_sb

    def normed_head(self, b, gain, gk, n, g0, rope_kind):
        x_, xk = self.xf()
        x_ = x_[:, 0:n]
        self.act(x_, self.ps(b, n), AF.Copy, [gk], ["ps%d" % b, xk], scale=gain)
        self.act(self.sq_t[:, 0:n], self.ps(b, n), AF.Square, [], ["ps%d" % b, "sq_t"])
        b2 = self.bank()
        self.mm(b2, self.ps(b2, n), self.ones_f, self.sq_t[:, 0:n], True, True, ["sq_t", "const3"])
        r, rk = self.rstd_from_sum(b2, n, 128.0, None)
        self.rope(x_, xk, n, g0, rope_kind)
        self.dve(V("tensor_tensor", out=x_, in0=x_, in1=r, op=ALU.mult), [xk, rk], [xk])
        return x_, xk

    def ph_kv(self, l):
        A, S, I, Sc = self.A, self.S, self.I, self.Sc
        hT_sb = self.load_hT(T)
        self.alloc_fm_scratch()
        wbuf = [A.tile(16 * 512, BF16).rearrange("p (k n) -> p k n", k=16) for _ in range(2)]
        wi = [0]

        def wnext(c0, ncols):
            i = wi[0] % 2
            wi[0] += 1
            wk = "w%d" % i
            self.wload(wbuf[i][:, :, 0:ncols], I["w_in"][l][:, c0:c0 + ncols], wk, 16, rkey="full_w_in%d" % l)
            return wbuf[i], wk

        gkg = A.tile(1, F32)
        self.load_T(gkg, I["gqa_k_g"][l:l + 1, :], 1, None, "gkg")
        gkv = A.tile(2, F32)
        self.load_T(gkv, I["mla_kv_g"][l].rearrange("(c p) -> c p", p=128), 2, None, "gkv")
        wkv = A.tile(2 * 1024, BF16).rearrange("p (k n) -> p k n", k=2)
        self.wload(wkv, I["mla_w_kv_up"][l], "wkv", 2, split=1, cast=True)
        kvn = A.tile(2 * T, BF16).rearrange("p (k t) -> p k t", k=2)
        vst = [A.tile(512, BF16) for _ in range(2)]

        def v_tokmajor(lhs3, lk, nk, rhs_fn, rk, vcol0, out_re=None):
            for tt in range(10):
                b = self.bank()
                for k in range(nk):
                    o_ = self.ps(b, 512)
                    if out_re is not None:
                        o_ = o_.rearrange("p (h d) -> p h d", h=4)
                    self.mm(b, o_, lhs3[:, k, tt * 128:(tt + 1) * 128], rhs_fn(k), k == 0, k == nk - 1,
                            [lk, rk])
                st = vst[tt % 2]
                sk = "vst%d" % (tt % 2)
                self.act(st, self.ps(b, 512), AF.Copy, [], ["ps%d" % b, sk])
                if tt < 8:
                    self.dma(Sc["pay_v"][tt * 128:(tt + 1) * 128, vcol0:vcol0 + 512], st, [sk], ["pay_v"], sk)
                else:
                    self.dma(Sc["vc"][(tt - 8) * 128:(tt - 7) * 128, vcol0:vcol0 + 512], st, [sk], ["vc"], sk)

        w, wk = wnext(0, 512)
        for j in range(2):
            for (g0, gn) in GROUPS:
                b = self.fm_tiles(w, wk, hT_sb, j * 128, 128, g0, gn)
                x_, xk = self.normed_head(b, gkg[:, 0:1], "gkg", gn, g0, "g")
                self.store_fm(x_, xk, 128, gn, g0, Sc["pay_k"], Sc["kc"], j * 128, "pay_k")
        for tt in range(10):
            b = self.bank()
            for k in range(16):
                self.mm(b, self.ps(b, 256), hT_sb[:, k, tt * 128:(tt + 1) * 128], w[:, k, 256:512], k == 0, k == 15,
                        ["hT_sb", wk])
            st = vst[tt % 2]
            sk = "vst%d" % (tt % 2)
            self.act(st[:, 0:256], self.ps(b, 256), AF.Copy, [], ["ps%d" % b, sk])
            if tt < 8:
                self.dma(Sc["pay_v"][tt * 128:(tt + 1) * 128, 0:256], st[:, 0:256], [sk], ["pay_v"], sk)
            else:
                self.dma(Sc["vc"][(tt - 8) * 128:(tt - 7) * 128, 0:256], st[:, 0:256], [sk], ["vc"], sk)
        w, wk = wnext(512, 320)
        for (g0, gn) in GROUPS:
            bs = [self.fm_tiles(w, wk, hT_sb, c * 128, 128, g0, gn) for c in range(2)]
            b2 = self.bank()
            xs = []
            for c in range(2):
                x_, xk = self.xf()
                x_ = x_[:, 0:gn]
                self.act(x_, self.ps(bs[c], gn), AF.Copy, ["gkv"], ["ps%d" % bs[c], xk], scale=gkv[:, c:c + 1])
                self.act(self.sq_t[:, 0:gn], self.ps(bs[c], gn), AF.Square, [], ["ps%d" % bs[c], "sq_t"])
                self.mm(b2, self.ps(b2, gn), self.ones_f, self.sq_t[:, 0:gn], c == 0, c == 1, ["sq_t", "const3"])
                xs.append((x_, xk))
            r, rk = self.rstd_from_sum(b2, gn, 256.0, None)
            for c in range(2):
                x_, xk = xs[c]
                self.dve(V("tensor_tensor", out=kvn[:, c, g0:g0 + gn], in0=x_, in1=r, op=ALU.mult),
                         [xk, rk], ["kvn"])
            b = self.fm_tiles(w, wk, hT_sb, 256, 64, g0, gn)
            x_, xk = self.xf()
            x_ = x_[0:64, 0:gn]
            self.act(x_, self.ps(b, gn, parts=64), AF.Copy, [], ["ps%d" % b, xk])
            self.rope(x_, xk, gn, g0, "6", parts=64)
            self.store_fm(x_, xk, 64, gn, g0, Sc["pay_k"], Sc["kc"], 768, "pay_k")
        for h in range(4):
            for (g0, gn) in GROUPS:
                b = self.bank()
                for k in range(2):
                    self.mm(b, self.ps(b, gn), wkv[:, k, h * 256:h * 256 + 128], kvn[:, k, g0:g0 + gn], k == 0, k == 1,
                            ["wkv", "kvn"])
                x_, xk = self.xf()
                x_ = x_[:, 0:gn]
                self.act(x_, self.ps(b, gn), AF.Copy, [], ["ps%d" % b, xk])
                self.store_fm(x_, xk, 128, gn, g0, Sc["pay_k"], Sc["kc"], 256 + h * 128, "pay_k")
        wkv_v = wkv.rearrange("p k (h two d) -> p k h two d", two=2, d=128)
        v_tokmajor(kvn, "kvn", 2, lambda k: wkv_v[:, k, :, 1, :], "wkv", 256, out_re=True)
        w, wk = wnext(832, 512)
        for h in range(4):
            for (g0, gn) in GROUPS:
                b = self.fm_tiles(w, wk, hT_sb, h * 128, 128, g0, gn)
                x_, xk = self.xf()
                x_ = x_[:, 0:gn]
                self.act(x_, self.ps(b, gn), AF.Copy, [], ["ps%d" % b, xk])
                self.rope(x_, xk, gn, g0, "6")
                self.store_fm(x_, xk, 128, gn, g0, Sc["pay_k"], Sc["kc"], 832 + h * 128, "pay_k")
        w, wk = wnext(1344, 512)
        v_tokmajor(hT_sb, "hT_sb", 16, lambda k: w[:, k, :], wk, 768)
        wa, wak = wnext(N_KV, 512)
        wg, wgk = wnext(N_KV + 512, 512)
        sst = [A.tile(512, F32) for _ in range(2)]
        for c in range(4):
            for gi, (g0, gn) in enumerate(GROUPS):
                ba = self.fm_tiles(wa, wak, hT_sb, c * 128, 128, g0, gn)
                bg = self.fm_tiles(wg, wgk, hT_sb, c * 128, 128, g0, gn)
                x_, xk = self.xf()
                x_ = x_[:, 0:gn]
                self.act(x_, self.ps(bg, gn), AF.Sigmoid, [], ["ps%d" % bg, xk])
                i = (c * 3 + gi) % 2
                s_ = sst[i][:, 0:gn]
                sk = "sst%d" % i
                self.dve(V("tensor_tensor", out=s_, in0=self.ps(ba, gn), in1=x_,
                                                                                op=ALU.mult),
                         [xk], ["ps%d" % ba, sk])
                self.dma(Sc["sT"][c * 128:(c + 1) * 128, g0:g0 + gn], s_, [sk], ["sT"], sk)
                if gi == 0:
                    self.dma(Sc["pay_h"][c * 128:(c + 1) * 128, 0:15], s_[:, 0:15], [sk], ["pay_h"], sk)
                if gi == 1:
                    self.dma(Sc["pay_h"][c * 128:(c + 1) * 128, 15:30], s_[:, 497:512], [sk], ["pay_h"], sk)
        for src, dst, rk in [("pay_k", "gk", "pay_k"), ("pay_v", "gv", "pay_v"), ("pay_h", "gh", "pay_h")]:
            self.S.op("pool", V("collective_compute",
                "AllGather", ALU.bypass, replica_groups=[list(range(NCORE))], ins=[Sc[src]], outs=[Sc[dst]]),
                reads=[rk], writes=[dst], dma_key="cc", inc=1)

    def ph_q(self, l, ngroups):
        A, S, I, Sc = self.A, self.S, self.I, self.Sc
        groups = GROUPS[:ngroups]
        ncols = T if ngroups == 3 else TL
        hT_sb = self.load_hT(ncols)
        self.alloc_fm_scratch()
        wbuf = [A.tile(16 * 512, BF16).rearrange("p (k n) -> p k n", k=16) for _ in range(2)]
        gqg = A.tile(1, F32)
        self.load_T(gqg, I["gqa_q_g"][l:l + 1, :], 1, None, "gqg")
        gql = A.tile(4, F32)
        self.load_T(gql, I["mla_q_g"][l].rearrange("(c p) -> c p", p=128), 4, None, "gql")
        wqn = A.tile(4 * 512, BF16).rearrange("p (k n) -> p k n", k=4)
        wqr = A.tile(4 * 256, BF16).rearrange("p (k n) -> p k n", k=4)
        wq_src = I["mla_w_q_up"][l].rearrange("(k p) (h d) -> p k h d", p=128, d=192)
        for k in range(4):
            self.dma(wqn[:, k, :].rearrange("p (h d) -> p h d", d=128), wq_src[:, k, :, 0:128], [], ["wqn"], "wqn",
                     eng="pool")
            self.dma(wqr[:, k, :].rearrange("p (h d) -> p h d", d=64), wq_src[:, k, :, 128:192], [], ["wqr"], "wqr",
                     eng="pool")
        qn = A.tile(4 * ncols, BF16).rearrange("p (k t) -> p k t", k=4)
        w, wk = wbuf[0], "w0"
        self.wload(w, I["w_in"][l][:, N_KV + 1024:N_KV + 1536], wk, 16, rkey="full_w_in%d" % l)
        for h in range(4):
            for (g0, gn) in groups:
                b = self.fm_tiles(w, wk, hT_sb, h * 128, 128, g0, gn)
                x_, xk = self.normed_head(b, gqg[:, 0:1], "gqg", gn, g0, "g")
                self.store_fm(x_, xk, 128, gn, 0, Sc["qT"][:, g0:g0 + gn], None, h * 128, "qT")
        w, wk = wbuf[1], "w1"
        self.wload(w, I["w_in"][l][:, N_KV + 1536:N_KV + 2048], wk, 16, rkey="full_w_in%d" % l)
        for (g0, gn) in groups:
            bs = [self.fm_tiles(w, wk, hT_sb, c * 128, 128, g0, gn) for c in range(4)]
            b2 = self.bank()
            xs = []
            for c in range(4):
                x_, xk = self.xf()
                x_ = x_[:, 0:gn]
                self.act(x_, self.ps(bs[c], gn), AF.Copy, ["gql"], ["ps%d" % bs[c], xk], scale=gql[:, c:c + 1])
                self.act(self.sq_t[:, 0:gn], self.ps(bs[c], gn), AF.Square, [], ["ps%d" % bs[c], "sq_t"])
                self.mm(b2, self.ps(b2, gn), self.ones_f, self.sq_t[:, 0:gn], c == 0, c == 3, ["sq_t", "const3"])
                xs.append((x_, xk))
            r, rk = self.rstd_from_sum(b2, gn, 512.0, None)
            for c in range(4):
                x_, xk = xs[c]
                self.dve(V("tensor_tensor", out=qn[:, c, g0:g0 + gn], in0=x_, in1=r,
                                                                              op=ALU.mult),
                         [xk, rk], ["qn"])
        for h in range(4):
            for (g0, gn) in groups:
                b = self.bank()
                for k in range(4):
                    self.mm(b, self.ps(b, gn), wqn[:, k, h * 128:(h + 1) * 128], qn[:, k, g0:g0 + gn], k == 0, k == 3,
                            ["wqn", "qn"])
                x_, xk = self.xf()
                x_ = x_[:, 0:gn]
                self.act(x_, self.ps(b, gn), AF.Copy, [], ["ps%d" % b, xk])
                self.store_fm(x_, xk, 128, gn, 0, Sc["qT"][:, g0:g0 + gn], None, 512 + h * 128, "qT")
        for pr in range(2):
            for (g0, gn) in groups:
                b = self.bank()
                for k in range(4):
                    self.mm(b, self.ps(b, gn), wqr[:, k, pr * 128:(pr + 1) * 128], qn[:, k, g0:g0 + gn], k == 0, k == 3,
                            ["wqr", "qn"])
                x_, xk = self.xf()
                x_ = x_[:, 0:gn]
                self.act(x_, self.ps(b, gn), AF.Copy, [], ["ps%d" % b, xk])
                self.rope(x_, xk, gn, g0, "6")
                self.store_fm(x_, xk, 128, gn, 0, Sc["qT"][:, g0:g0 + gn], None, 1024 + pr * 128, "qT")
        w, wk = wbuf[0], "w0"
        self.wload(w, I["w_in"][l][:, N_KV + 2048:N_KV + 2560], wk, 16, rkey="full_w_in%d" % l)
        for h in range(4):
            for (g0, gn) in groups:
                b = self.fm_tiles(w, wk, hT_sb, h * 128, 128, g0, gn)
                x_, xk = self.xf()
                x_ = x_[:, 0:gn]
                self.act(x_, self.ps(b, gn), AF.Copy, [], ["ps%d" % b, xk])
                self.rope(x_, xk, gn, g0, "6")
                self.store_fm(x_, xk, 128, gn, 0, Sc["qT"][:, g0:g0 + gn], None, 1280 + h * 128, "qT")

    def ph_att(self, l, ngroups):
        A, S, I, Sc = self.A, self.S, self.I, self.Sc
        if l == 0:
            self.gather_w([("w_branch", 0), ("w_out", 0), ("ffn_w_up", 0), ("ffn_w_down", 0)])
        groups = GROUPS[:ngroups]
        lam_init = 0.8 - 0.6 * math.exp(-0.3 * l)
        lt = A.tile(4 * 64, F32).rearrange("p (a d) -> p a d", a=4)
        for i, nm in enumerate(["diff_lq1", "diff_lk1", "diff_lq2", "diff_lk2"]):
            self.dma(lt[:, i, :], I[nm][l].partition_broadcast(128), [], ["lt"], "lt")
        lw = A.tile(8, F32)
        lp = A.tile(64, F32)
        for j in range(2):
            self.dve(V("tensor_tensor", out=lp, in0=lt[:, 2 * j, :], in1=lt[:, 2 * j + 1, :], op=ALU.mult),
                     ["lt"], ["lp"])
            self.dve(V("reduce_sum", out=lw[:, j:j + 1], in_=lp, axis=AX.X), ["lp"], ["lw%d" % j])
            self.act(lw[:, 2 + j:3 + j], lw[:, j:j + 1], AF.Exp, ["lw%d" % j], ["le%d" % j])
        self.dve(V("tensor_tensor", out=lw[:, 4:5], in0=lw[:, 3:4], in1=lw[:, 2:3], op=ALU.subtract),
                 ["le0", "le1"], ["nl0"])
        self.dve(V("tensor_scalar_add", out=lw[:, 5:6], in0=lw[:, 4:5], scalar1=-lam_init), ["nl0"], ["nlam"])
        nlam = lw[:, 5:6]
        gd = A.tile(2, F32)
        self.load_T(gd[:, 0:1], I["diff_g"][l:l + 1, :], 1, None, "gd0")
        self.dve(V("tensor_scalar_mul", out=gd[:, 1:2], in0=gd[:, 0:1], scalar1=1.0 - lam_init), ["gd0"], ["gd"])
        self.rs_t = [A.tile(512, F32) for _ in range(2)]
        self.rs_i = 0
        self.sq_t = A.tile(512, F32)
        rec = [A.tile(512, F32) for _ in range(2)]
        of = [A.tile(512, F32) for _ in range(2)]
        ost = [A.tile(512, BF16) for _ in range(2)]
        qb = [A.tile(512, BF16) for _ in range(4)]
        kmain = [A.tile(1024, BF16) for _ in range(2)]
        krope = [A.tile(1024, BF16) for _ in range(2)]
        vp = [A.tile(8 * 128, BF16).rearrange("p (c d) -> p c d", c=8) for _ in range(2)]
        NPT = 6
        LOOK = 2
        pt = [A.tile(512, BF16) for _ in range(NPT)]
        acc = [[A.tile(512, F32) for _ in range(2)] for _ in range(2)]
        pi = [0]
        pci = [0]
        ost_i = [0]

        jobs = []
        for kvh in range(2):
            jobs.append(dict(kind="gqa", krow=kvh * 128, vcol=kvh * 128, scale=128 ** -0.5,
                             streams=[dict(q=(2 * kvh + s) * 128, out=512 + (2 * kvh + s) * 128) for s in range(2)]))
        for h in range(4):
            jobs.append(dict(kind="mla", krow=256 + h * 128, vcol=256 + h * 128, scale=192 ** -0.5,
                             streams=[dict(q=512 + h * 128, qr=1024 + (h // 2) * 128 + (h % 2) * 64,
                                           out=1024 + h * 128)]))
        for h in range(4):
            jobs.append(dict(kind="diff", krow=832 + h * 128, vcol=768 + h * 128, scale=64 ** -0.5,
                             streams=[dict(q=1280 + h * 128, half=0), dict(q=1280 + h * 128, half=1)],
                             out=1536 + h * 128))

        for job in jobs:
            kind = job["kind"]
            ns = len(job["streams"])
            for (g0, gn) in groups:
                qk = []
                for si, st in enumerate(job["streams"]):
                    if kind == "diff" and si == 1:
                        qk.append(qk[0])
                        continue
                    qt = qb[2 * si]
                    k_ = "qb%d" % (2 * si)
                    self.dma(qt[:, 0:gn], Sc["qT"][st["q"]:st["q"] + 128, g0:g0 + gn], ["qT"], [k_], k_)
                    ent = [(qt, k_)]
                    if kind == "mla":
                        qr = qb[2 * si + 1]
                        k2 = "qb%d" % (2 * si + 1)
                        self.dma(qr[0:64, 0:gn], Sc["qT"][st["qr"]:st["qr"] + 64, g0:g0 + gn], ["qT"], [k2], k2)
                        ent.append((qr, k2))
                    qk.append(ent)
                ob = [self.bank() for _ in range(ns)]
                sb = []
                pieces = [("ctx", 0, 2)] if g0 >= TL else [("ctx", 0, 2)] + [("lat", r, 8) for r in range(NCORE)]
                seq = []
                for pidx, (src, r, nch) in enumerate(pieces):
                    for c in range(nch):
                        for si in range(ns):
                            seq.append((pidx, c, si))
                n_items = len(seq)
                last_idx = {si: max(i for i, it in enumerate(seq) if it[2] == si) for si in range(ns)}
                loaded = {}

                def ensure(pidx):
                    if pidx in loaded or pidx >= len(pieces):
                        return
                    src, r, nch = pieces[pidx]
                    i = pci[0] % 2
                    pci[0] += 1
                    km, kr, v_ = kmain[i], krope[i], vp[i]
                    kk = "kp%d" % i
                    nkeys = nch * 128
                    if src == "ctx":
                        ksrc, vsrc, r0, v0 = Sc["kc"], Sc["vc"], 0, 0
                    else:
                        ksrc, vsrc, r0, v0 = Sc["gk"], Sc["gv"], r * KROWS, r * TL
                    rk_ = ["kc", "vc"] if src == "ctx" else ["gk", "gv"]
                    self.dma(km[:, 0:nkeys], ksrc[r0 + job["krow"]:r0 + job["krow"] + 128, 0:nkeys], rk_, [kk], kk)
                    if kind == "mla":
                        self.dma(kr[0:64, 0:nkeys], ksrc[r0 + 768:r0 + 832, 0:nkeys], rk_, [kk], kk)
                    self.dma(v_[:, 0:nch, :],
                             vsrc[v0:v0 + nkeys, job["vcol"]:job["vcol"] + 128].rearrange("(c p) d -> p c d", p=128),
                             rk_, [kk], kk)
                    loaded[pidx] = (km, kr, v_, kk)

                sbanks = {}

                def emit_s(ii):
                    pidx, c, si = seq[ii]
                    km, kr, v_, kk = loaded[pidx]
                    b = self.bank()
                    while b in ob or b in sb:
                        b = self.bank()
                    sbanks[ii] = b
                    ent = qk[si]
                    if kind == "gqa":
                        self.mm(b, self.ps(b, gn), km[:, c * 128:(c + 1) * 128], ent[0][0][:, 0:gn], True, True,
                                [kk, ent[0][1]])
                    elif kind == "mla":
                        self.mm(b, self.ps(b, gn), km[:, c * 128:(c + 1) * 128], ent[0][0][:, 0:gn], True, False,
                                [kk, ent[0][1]])
                        self.mm(b, self.ps(b, gn), kr[0:64, c * 128:(c + 1) * 128], ent[1][0][0:64, 0:gn], False, True,
                                [kk, ent[1][1]])
                    else:
                        hf_ = job["streams"][si]["half"]
                        self.mm(b, self.ps(b, gn), km[hf_ * 64:(hf_ + 1) * 64, c * 128:(c + 1) * 128],
                                ent[0][0][hf_ * 64:(hf_ + 1) * 64, 0:gn], True, True, [kk, ent[0][1]])

                first = [True] * ns
                ensure(0)
                ensure(1)
                nxt = [0]

                def pump(upto):
                    while nxt[0] <= min(upto, n_items - 1):
                        emit_s(nxt[0])
                        nxt[0] += 1

                cur_piece = 0
                cnt = [0] * ns
                for ii in range(n_items):
                    pidx, c, si = seq[ii]
                    if pidx != cur_piece:
                        cur_piece = pidx
                        ensure(pidx + 1)
                    pump(ii + LOOK)
                    km, kr, v_, kk = loaded[pidx]
                    b = sbanks.pop(ii)
                    p_ = pt[pi[0] % NPT]
                    pk = "pt%d" % (pi[0] % NPT)
                    pi[0] += 1
                    self.act(p_[:, 0:gn], self.ps(b, gn), AF.Exp, [], ["ps%d" % b, pk], scale=job["scale"])
                    last_ = last_idx[si] == ii
                    self.mm(ob[si], self.ps(ob[si], gn), v_[:, c, :], p_[:, 0:gn], first[si], last_, [kk, pk])
                    first[si] = False
                    par = cnt[si] % 2
                    a_ = acc[si][par][:, 0:gn]
                    ak = "acc%d_%d" % (si, par)
                    if cnt[si] < 2:
                        self.dve(V("tensor_copy", out=a_, in_=p_[:, 0:gn]), [pk], [ak])
                    else:
                        self.dve(V("tensor_tensor", out=a_, in0=a_, in1=p_[:, 0:gn], op=ALU.add), [pk, ak], [ak])
                    cnt[si] += 1
                for si in range(ns):
                    a0 = acc[si][0][:, 0:gn]
                    if cnt[si] > 1:
                        self.dve(V("tensor_tensor", out=a0, in0=a0, in1=acc[si][1][:, 0:gn], op=ALU.add),
                                 ["acc%d_0" % si, "acc%d_1" % si], ["acc%d_0" % si])
                    b = self.bank()
                    while b in ob or b in sb:
                        b = self.bank()
                    sb.append(b)
                    self.mm(b, self.ps(b, gn), self.ones_f, a0, True, True, ["acc%d_0" % si, "const3"])
                outs = []
                for si in range(ns):
                    r_ = rec[si][:, 0:gn]
                    rk = "rec%d" % si
                    self.dve(V("reciprocal", out=r_, in_=self.ps(sb[si], gn)), [],
                             ["ps%d" % sb[si], rk])
                    o_ = of[si][:, 0:gn]
                    ok = "of%d" % si
                    self.dve(V("tensor_tensor", out=o_, in0=self.ps(ob[si], gn), in1=r_,
                                                                            op=ALU.mult),
                             [rk], ["ps%d" % ob[si], ok])
                    outs.append((o_, ok))
                if kind == "diff":
                    o1, k1 = outs[0]
                    o2, k2 = outs[1]
                    self.dve(V("scalar_tensor_tensor", out=o1, in0=o2, scalar=nlam, in1=o1,
                                                                            op0=ALU.mult, op1=ALU.add),
                             [k1, k2, "nlam"], [k1])
                    self.act(self.sq_t[:, 0:gn], o1, AF.Square, [k1], ["sq_t"])
                    b2 = self.bank()
                    self.mm(b2, self.ps(b2, gn), self.ones_f, self.sq_t[:, 0:gn], True, True, ["sq_t", "const3"])
                    r, rk = self.rstd_from_sum(b2, gn, 128.0, None)
                    self.dve(V("scalar_tensor_tensor", out=o1, in0=o1, scalar=gd[:, 1:2], in1=r,
                                                                          op0=ALU.mult, op1=ALU.mult),
                             [k1, rk, "gd"], [k1])
                    fin = [(o1, k1, job["out"])]
                else:
                    fin = [(outs[si][0], outs[si][1], job["streams"][si]["out"]) for si in range(ns)]
                for (o_, ok, orow) in fin:
                    st = ost[ost_i[0] % 2]
                    sk = "ost%d" % (ost_i[0] % 2)
                    ost_i[0] += 1
                    self.act(st[:, 0:gn], o_, AF.Copy, [ok], [sk])
                    self.dma(Sc["brT"][orow:orow + 128, g0:g0 + gn], st[:, 0:gn], [sk], ["brT"], sk)

    def ph_conv(self, l, with_ctx):
        A, S, I, Sc = self.A, self.S, self.I, self.Sc
        if l == 0:
            self.gather_w([("w_in", 1), ("w_branch", 1), ("w_out", 1), ("ffn_w_up", 1), ("ffn_w_down", 1)])
        cw = A.tile(4 * 31, F32).rearrange("p (c k) -> p c k", c=4)
        for c in range(4):
            self.load_T(cw[:, c, :], I["conv_w"][l][:, c * 128:(c + 1) * 128], 31, None, "cw")
        cv = A.tile(12, F32).rearrange("p (a c) -> p a c", a=3)
        for i, nm in enumerate(["conv_b", "conv_ln_g", "conv_ln_b"]):
            self.load_T(cv[:, i, :], I[nm][l].rearrange("(c p) -> c p", p=128), 4, None, "cv")
        segs = [(0, TL)] + ([(TL, TC)] if with_ctx else [])
        NE = TL + 30
        s_ext = A.tile(4 * NE, F32).rearrange("p (c t) -> p c t", c=4)
        ghs = A.tile(4 * 8 * 30, F32).rearrange("p (c r j) -> p c r j", c=4, r=8)
        u = A.tile(4 * TL, F32).rearrange("p (c t) -> p c t", c=4)
        sq = A.tile(512, F32)
        mean = A.tile(512, F32)
        msq = A.tile(512, F32)
        rstd = A.tile(512, F32)
        tt_ = A.tile(512, F32)
        yst = [A.tile(512, BF16) for _ in range(2)]
        yi = 0
        for (t0, n) in segs:
            ne = n + 30
            if t0 == 0:
                for c in range(4):
                    self.dma(ghs[:, c, :, :], Sc["gh"].rearrange("(r c p) j -> c p r j", c=4, p=128)[c], ["gh"], ["ghs"],
                             "ghs")
                for side in range(2):
                    dst = s_ext[:, :, 0:15] if side == 0 else s_ext[:, :, 15 + n:30 + n]
                    j0 = 15 if side == 0 else 0
                    for r in range(8):
                        src = ghs[:, :, r, j0:j0 + 15]
                        sc_ = self.sel[:, side * 8 + r:side * 8 + r + 1]
                        if r == 0:
                            self.dve(V("tensor_scalar_mul", out=dst, in0=src,
                                                                                               scalar1=sc_),
                                     ["ghs", "const"], ["s_halo%d" % side])
                        else:
                            self.dve(V("scalar_tensor_tensor",
                                out=dst, in0=src, scalar=sc_, in1=dst, op0=ALU.mult, op1=ALU.add),
                                ["ghs", "const", "s_halo%d" % side], ["s_halo%d" % side])
            else:
                self.dve(V("memset", s_ext[:, :, 0:15], 0.0), [], ["s_ext", "s_halo0"])
                self.dve(V("memset", s_ext[:, :, 15 + n:30 + n], 0.0), [], ["s_ext", "s_halo1"])
            self.dma(s_ext[:, :, 15:15 + n], Sc["sT"][:, t0:t0 + n].rearrange("(c p) t -> p c t", p=128), ["sT"],
                     ["s_ext"], "s_ext")
            for c in range(4):
                eng = "dve"
                uk = "u%d" % c
                self.dve(V("tensor_scalar", out=u[:, c, 0:n], in0=s_ext[:, c, 0:n],
                                                              scalar1=cw[:, c, 0:1], scalar2=cv[:, 0, c:c + 1],
                                                              op0=ALU.mult, op1=ALU.add),
                         ["s_ext", "s_halo0", "s_halo1", "cw", "cv"], [uk], eng=eng)
                for k in range(1, 31):
                    self.dve(V("scalar_tensor_tensor", out=u[:, c, 0:n], in0=s_ext[:, c, k:k + n],
                                                                               scalar=cw[:, c, k:k + 1], in1=u[:, c, 0:n],
                                                                               op0=ALU.mult, op1=ALU.add),
                             ["s_ext", "s_halo0", "s_halo1", "cw", uk], [uk], eng=eng)
            for g0 in range(0, n, 512):
                gn = min(512, n - g0)
                b1 = self.bank()
                b2 = self.bank()
                for c in range(4):
                    self.mm(b1, self.ps(b1, gn), self.ones_f, u[:, c, g0:g0 + gn], c == 0, c == 3, ["u%d" % c, "const3"])
                for c in range(4):
                    self.act(sq[:, 0:gn], u[:, c, g0:g0 + gn], AF.Square, ["u%d" % c], ["sq"])
                    self.mm(b2, self.ps(b2, gn), self.ones_f, sq[:, 0:gn], c == 0, c == 3, ["sq", "const3"])
                self.act(mean[:, 0:gn], self.ps(b1, gn), AF.Copy, [], ["ps%d" % b1, "mean"], scale=1.0 / 512)
                self.dve(V("tensor_tensor", out=msq[:, 0:gn], in0=mean[:, 0:gn], in1=mean[:, 0:gn],
                                                           op=ALU.mult), ["mean"], ["msq"])
                self.dve(V("scalar_tensor_tensor", out=msq[:, 0:gn], in0=self.ps(b2, gn),
                                                                         scalar=1.0 / 512, in1=msq[:, 0:gn],
                                                                         op0=ALU.mult, op1=ALU.subtract),
                         ["msq"], ["ps%d" % b2, "msq"])
                self.act(rstd[:, 0:gn], msq[:, 0:gn], AF.Sqrt, ["msq"], ["rstd"], bias=self.epsc[:, 0:1])
                self.dve(V("reciprocal", out=rstd[:, 0:gn], in_=rstd[:, 0:gn]), ["rstd"], ["rstd"])
                for c in range(4):
                    self.dve(V("tensor_tensor", out=tt_[:, 0:gn], in0=u[:, c, g0:g0 + gn],
                                                                           in1=mean[:, 0:gn], op=ALU.subtract),
                             ["u%d" % c, "mean"], ["tt_"])
                    self.dve(V("tensor_tensor", out=tt_[:, 0:gn], in0=tt_[:, 0:gn], in1=rstd[:, 0:gn],
                                                               op=ALU.mult), ["tt_", "rstd"], ["tt_"])
                    st = yst[yi % 2]
                    sk = "yst%d" % (yi % 2)
                    yi += 1
                    self.act(st[:, 0:gn], tt_[:, 0:gn], AF.Silu, ["tt_", "cv"], [sk], scale=cv[:, 1, c:c + 1],
                             bias=cv[:, 2, c:c + 1])
                    self.dma(Sc["brT"][c * 128:(c + 1) * 128, t0 + g0:t0 + g0 + gn], st[:, 0:gn], [sk], ["brT"], sk)

    def ph_merge(self, l, ngroups):
        A, S, I, Sc = self.A, self.S, self.I, self.Sc
        groups = GROUPS[:ngroups]
        ncols = T if ngroups == 3 else TL
        hT_sb = self.load_hT(ncols)
        br = A.tile(16 * ncols, BF16).rearrange("p (k t) -> p k t", k=16)
        self.dma(br, Sc["brT"][:, 0:ncols].rearrange("(k p) t -> p k t", p=128), ["brT"], ["br"], "br")
        wg = [A.tile(4 * 16 * 128, BF16).rearrange("p (n k c) -> p n k c", n=4, k=16) for _ in range(2)]
        wb = [A.tile(4 * 4 * 128, BF16).rearrange("p (n k c) -> p n k c", n=4, k=4) for _ in range(2)]
        sg = [A.tile(512, F32) for _ in range(2)]
        m_ = A.tile(512, F32)
        t_ = A.tile(512, F32)
        mst = [A.tile(512, BF16) for _ in range(2)]
        si = 0
        mi = 0
        c_g = N_KV + 2560
        for j in range(16):
            i = j % 2
            wk = "wg%d" % i
            for n in range(4):
                src = I["w_in"][l][:, c_g + n * D + j * 128:c_g + n * D + (j + 1) * 128]
                self.dma(wg[i][:, n, :, :], src.rearrange("(k p) c -> p k c", p=128), ["full_w_in%d" % l], [wk], wk)
                srcb = I["w_branch"][l, n][:, j * 128:(j + 1) * 128]
                self.dma(wb[i][:, n, :, :], srcb.rearrange("(k p) c -> p k c", p=128), ["full_w_branch%d" % l], [wk], wk)
            for (g0, gn) in groups:
                for n in range(4):
                    bg = self.bank()
                    for k in range(16):
                        self.mm(bg, self.ps(bg, gn), wg[i][:, n, k, :], hT_sb[:, k, g0:g0 + gn], k == 0, k == 15,
                                [wk, "hT_sb"])
                    bp = self.bank()
                    for k in range(4):
                        self.mm(bp, self.ps(bp, gn), wb[i][:, n, k, :], br[:, n * 4 + k, g0:g0 + gn], k == 0, k == 3,
                                [wk, "br"])
                    s_ = sg[si % 2][:, 0:gn]
                    sk = "sg%d" % (si % 2)
                    si += 1
                    self.act(s_, self.ps(bg, gn), AF.Sigmoid, [], ["ps%d" % bg, sk])
                    if n == 0:
                        self.dve(V("tensor_tensor", out=m_[:, 0:gn], in0=self.ps(bp, gn),
                                                                                 in1=s_, op=ALU.mult),
                                 [sk], ["ps%d" % bp, "m_"])
                    else:
                        self.dve(V("tensor_tensor", out=t_[:, 0:gn], in0=self.ps(bp, gn),
                                                                                 in1=s_, op=ALU.mult),
                                 [sk], ["ps%d" % bp, "t_"])
                        self.dve(V("tensor_tensor", out=m_[:, 0:gn], in0=m_[:, 0:gn], in1=t_[:, 0:gn],
                                                                   op=ALU.add), ["t_", "m_"], ["m_"])
                st = mst[mi % 2]
                stk = "mst%d" % (mi % 2)
                mi += 1
                self.act(st[:, 0:gn], m_[:, 0:gn], AF.Copy, ["m_"], [stk])
                self.dma(Sc["mT"][j * 128:(j + 1) * 128, g0:g0 + gn], st[:, 0:gn], [stk], ["mT"], stk)

    def resid_update(self, l, goff, tiles, lhs_fn, nk, rhs_fn, rhs_keys_fn, lhs_key, cg_outer=None):
        pass

    def ph_out(self, l, ntiles):
        A, S, I, Sc = self.A, self.S, self.I, self.Sc
        wo = A.tile(16 * D, BF16).rearrange("p (k n) -> p k n", k=16)
        for q in range(4):
            self.wload(wo[:, :, q * 512:(q + 1) * 512], I["w_out"][l][:, q * 512:(q + 1) * 512], "wo", 16,
                       rkey="full_w_out%d" % l)
        G = []
        for r in range(2 if ntiles > 8 else 1):
            g_ = A.tile(D, F32)
            self.bcast_load(g_, Sc["modv"][l, r, 2 * D:3 * D], "G%d" % r)
            G.append(g_)
        xt = [A.tile(D, F32) for _ in range(2)]
        mt = [A.tile(16 * 128, BF16).rearrange("p (k t) -> p k t", k=16) for _ in range(2)]
        t_ = A.tile(512, F32)
        for tt in range(ntiles):
            i = tt % 2
            r = 0 if tt < 8 else 1
            xk, mk = "xt%d" % i, "mt%d" % i
            self.dma(xt[i], Sc["xres"][tt * 128:(tt + 1) * 128, :], ["xres"], [xk], xk)
            self.dma(mt[i], Sc["mT"][:, tt * 128:(tt + 1) * 128].rearrange("(k p) t -> p k t", p=128), ["mT"], [mk], mk)
            for cg in range(4):
                b = self.bank()
                for k in range(16):
                    self.mm(b, self.ps(b, 512), mt[i][:, k, :], wo[:, k, cg * 512:(cg + 1) * 512], k == 0, k == 15,
                            [mk, "wo"])
                self.dve(V("tensor_tensor", out=t_, in0=self.ps(b, 512),
                                                                    in1=G[r][:, cg * 512:(cg + 1) * 512], op=ALU.mult),
                         ["G%d" % r], ["ps%d" % b, "t_"])
                self.dve(V("tensor_tensor", out=xt[i][:, cg * 512:(cg + 1) * 512],
                                                                in0=xt[i][:, cg * 512:(cg + 1) * 512], in1=t_,
                                                                op=ALU.add), ["t_", xk], [xk])
            self.dma(Sc["xres"][tt * 128:(tt + 1) * 128, :], xt[i], [xk], ["xres"], xk)

    def ph_ffn(self, l, with_ctx):
        A, S, I, Sc = self.A, self.S, self.I, self.Sc
        NE = TL + 2
        h2 = A.tile(16 * NE, BF16).rearrange("p (k t) -> p k t", k=16)
        self.dma(h2[:, :, 1:1 + TL], Sc["hT"][:, :, 0:TL].rearrange("k p t -> p k t"), ["hT"], ["h2"], "h2")
        g2s = A.tile(16 * 16, BF16).rearrange("p (k r j) -> p k r j", k=16, r=8)
        for k in range(16):
            self.dma(g2s[:, k, :, :], Sc["gh2"].rearrange("(r k p) j -> k p r j", k=16, p=128)[k], ["gh2"], ["g2s"], "g2s")
        hacc = A.tile(32, F32).rearrange("p (s k) -> p s k", s=2)
        for side in range(2):
            j = 1 if side == 0 else 0
            for r in range(8):
                sc_ = self.sel[:, side * 8 + r:side * 8 + r + 1]
                src = g2s[:, :, r, j]
                dst = hacc[:, side, :]
                if r == 0:
                    self.dve(V("tensor_scalar_mul", out=dst, in0=src, scalar1=sc_),
                             ["g2s", "const"], ["hacc%d" % side])
                else:
                    self.dve(V("scalar_tensor_tensor",
                        out=dst, in0=src, scalar=sc_, in1=dst, op0=ALU.mult, op1=ALU.add),
                        ["g2s", "const", "hacc%d" % side], ["hacc%d" % side])
            col = 0 if side == 0 else NE - 1
            self.dve(V("tensor_copy", out=h2[:, :, col], in_=hacc[:, side, :]),
                     ["hacc%d" % side], ["h2e%d" % side])
        h2k = ["h2", "h2e0", "h2e1"]
        if with_ctx:
            NC_ = TC + 2
            h2c = A.tile(16 * NC_, BF16).rearrange("p (k t) -> p k t", k=16)
            self.dve(V("memset", h2c[:, :, 0:1], 0.0), [], ["h2c0"])
            self.dve(V("memset", h2c[:, :, NC_ - 1:NC_], 0.0), [], ["h2c1"])
            self.dma(h2c[:, :, 1:1 + TC], Sc["hT"][:, :, TL:T].rearrange("k p t -> p k t"), ["hT"], ["h2c"], "h2c")
            h2ck = ["h2c", "h2c0", "h2c1"]
        fw = A.tile(88 * 3, F32).rearrange("p (t k) -> p t k", k=3)
        fb = A.tile(88, F32)
        for k in range(3):
            self.load_T(fw[:, :, k], I["ffn_dw_w"][l, k].rearrange("(t p) -> t p", p=128), 88, None, "fw")
        self.load_T(fb, I["ffn_dw_b"][l].rearrange("(t p) -> t p", p=128), 88, None, "fb")
        wbuf = [[A.tile(16 * 512, BF16).rearrange("p (k n) -> p k n", k=16) for _ in range(2)] for _ in range(2)]
        ue = [A.tile(NE, F32) for _ in range(2)]
        y = [A.tile(TL, F32) for _ in range(2)]
        ast = [A.tile(TL, BF16) for _ in range(2)]
        ai = 0
        segs = [(0, TL, h2, h2k)] + ([(TL, TC, h2c, h2ck)] if with_ctx else [])
        for it in range(44):
            if it % 4 == 0:
                wi = (it // 4) % 2
                wk = "fw%d" % wi
                for ab in range(2):
                    c0 = ab * DFF + it * 128
                    self.wload(wbuf[wi][ab], I["ffn_w_up"][l][:, c0:c0 + 512], wk, 16, rkey="full_ffn_w_up%d" % l)
            co = (it % 4) * 128
            for (t0, n, hsrc, hkeys) in segs:
                ne = n + 2
                for ab in range(2):
                    ti = it + ab * 44
                    u_ = ue[ab]
                    uk = "ue%d" % ab
                    for c0 in range(0, ne, 342):
                        cn = min(342, ne - c0)
                        b = self.bank()
                        for k in range(16):
                            self.mm(b, self.ps(b, cn), wbuf[wi][ab][:, k, co:co + 128], hsrc[:, k, c0:c0 + cn],
                                    k == 0, k == 15, [wk] + hkeys)
                        self.act(u_[:, c0:c0 + cn], self.ps(b, cn), AF.Copy, [], ["ps%d" % b, uk])
                    y_ = y[ab][:, 0:n]
                    yk = "y%d" % ab
                    self.dve(V("tensor_scalar", out=y_, in0=u_[:, 0:n],
                                                                                 scalar1=fw[:, ti, 0:1],
                                                                                 scalar2=fb[:, ti:ti + 1],
                                                                                 op0=ALU.mult, op1=ALU.add),
                             [uk, "fw", "fb"], [yk])
                    for k in (1, 2):
                        self.dve(V("scalar_tensor_tensor",
                            out=y_, in0=u_[:, k:k + n], scalar=fw[:, ti, k:k + 1], in1=y_, op0=ALU.mult, op1=ALU.add),
                            [uk, "fw", yk], [yk])
                self.act(y[0][:, 0:n], y[0][:, 0:n], AF.Silu, ["y0"], ["y0"])
                st = ast[ai % 2]
                sk = "ast%d" % (ai % 2)
                ai += 1
                self.dve(V("tensor_tensor", out=st[:, 0:n], in0=y[0][:, 0:n], in1=y[1][:, 0:n],
                                                                op=ALU.mult), ["y0", "y1"], [sk])
                self.dma(Sc["actT"][it * 128:(it + 1) * 128, t0:t0 + n], st[:, 0:n], [sk], ["actT"], sk)

    def ph_down(self, l, with_ctx):
        A, S, I, Sc = self.A, self.S, self.I, self.Sc
        G = []
        for r in range(2 if with_ctx else 1):
            g_ = A.tile(D, F32)
            self.bcast_load(g_, Sc["modv"][l, r, 5 * D:6 * D], "G%d" % r)
            G.append(g_)
        at = A.tile(44 * 512, BF16).rearrange("p (k t) -> p k t", k=44)
        CW = 256
        wd = [A.tile(44 * CW, BF16).rearrange("p (k n) -> p k n", k=44) for _ in range(2)]
        xt = A.tile(4 * D, F32).rearrange("p (a d) -> p a d", a=4)
        t_ = A.tile(CW, F32)
        tgs = [(0, 512, 0), (512, 512, 0)] + ([(TL, TC, 1)] if with_ctx else [])
        wi = 0
        for (t0, n, r) in tgs:
            ntl = n // 128
            self.dma(at[:, :, 0:n], Sc["actT"][:, t0:t0 + n].rearrange("(k p) t -> p k t", p=128), ["actT"], ["at"], "at")
            for a in range(ntl):
                self.dma(xt[:, a, :], Sc["xres"][t0 + a * 128:t0 + (a + 1) * 128, :], ["xres"], ["xt"], "xt")
            for cg in range(D // CW):
                w = wd[wi % 2]
                wk = "wd%d" % (wi % 2)
                wi += 1
                self.wload(w, I["ffn_w_down"][l][:, cg * CW:(cg + 1) * CW], wk, 44, split=11,
                           rkey="full_ffn_w_down%d" % l)
                for a in range(ntl):
                    b = self.bank()
                    for k in range(44):
                        self.mm(b, self.ps(b, CW), at[:, k, a * 128:(a + 1) * 128], w[:, k, :], k == 0, k == 43,
                                ["at", wk])
                    self.dve(V("tensor_tensor", out=t_, in0=self.ps(b, CW),
                                                                        in1=G[r][:, cg * CW:(cg + 1) * CW],
                                                                        op=ALU.mult),
                             ["G%d" % r], ["ps%d" % b, "t_"])
                    self.dve(V("tensor_tensor", out=xt[:, a, cg * CW:(cg + 1) * CW],
                                                                    in0=xt[:, a, cg * CW:(cg + 1) * CW], in1=t_,
                                                                    op=ALU.add), ["t_", "xt"], ["xt"])
            for a in range(ntl):
                self.dma(Sc["xres"][t0 + a * 128:t0 + (a + 1) * 128, :], xt[:, a, :], ["xt"], ["xres"], "xt")

    def ph_final(self, out):
        self.ph_norm(0, 1, 8, final_out=out)


def _rope_np(row, col, rot_dim):
    axis_dim = rot_dim // 2
    inv = np.power(np.float32(10000.0), -np.arange(0, axis_dim, 2, dtype=np.float32) / np.float32(axis_dim)).astype(np.float32)
    ang = np.concatenate([row.astype(np.float32)[:, None] * inv, col.astype(np.float32)[:, None] * inv], axis=-1)
    return np.cos(ang).astype(np.float32), np.sin(ang).astype(np.float32)


def _consts(core):
    t = np.arange(core * TL, (core + 1) * TL)
    row, col = t // 64, t % 64
    cg, sg = _rope_np(row, col, 128)
    c6, s6 = _rope_np(row, col, 64)
    ropeg_c = np.ones((128, T), np.float32)
    ropeg_s = np.zeros((128, T), np.float32)
    ropeg_c[0:64, 0:TL] = cg.T
    ropeg_c[64:128, 0:TL] = cg.T
    ropeg_s[0:64, 0:TL] = -sg.T
    ropeg_s[64:128, 0:TL] = sg.T
    rope6_c = np.ones((128, T), np.float32)
    rope6_s = np.zeros((128, T), np.float32)
    for blk in range(2):
        o = blk * 64
        rope6_c[o:o + 32, 0:TL] = c6.T
        rope6_c[o + 32:o + 64, 0:TL] = c6.T
        rope6_s[o:o + 32, 0:TL] = -s6.T
        rope6_s[o + 32:o + 64, 0:TL] = s6.T
    sel = np.zeros((128, 16), np.float32)
    if core > 0:
        sel[:, core - 1] = 1.0
    if core < NCORE - 1:
        sel[:, 8 + core + 1] = 1.0
    ident = np.eye(128, dtype=np.float32)
    perm64 = np.zeros((128, 128), np.float32)
    perm32 = np.zeros((128, 128), np.float32)
    for i in range(128):
        perm64[i, (i + 64) % 128] = 1.0
        perm32[i, i ^ 32] = 1.0
    return dict(ropeg_c=ropeg_c, ropeg_s=ropeg_s, rope6_c=rope6_c, rope6_s=rope6_s, sel=sel, ident_f=ident,
                perm64=perm64, perm32=perm32)


WEIGHT_NAMES = ["w_mod", "b_mod", "norm1_g", "norm2_g", "w_in", "conv_w", "conv_b", "conv_ln_g", "conv_ln_b",
                "gqa_q_g", "gqa_k_g", "mla_q_g", "mla_w_q_up", "mla_kv_g", "mla_w_kv_up", "diff_lq1", "diff_lk1",
                "diff_lq2", "diff_lk2", "diff_g", "w_branch", "w_out", "ffn_w_up", "ffn_dw_w", "ffn_dw_b",
                "ffn_w_down", "final_g"]


def make_in_maps(inputs):
    x = np.ascontiguousarray(np.asarray(inputs["x"], dtype=np.float32)[0])
    ctx = np.ascontiguousarray(np.asarray(inputs["ctx"], dtype=np.float32)[0])
    cc = np.ascontiguousarray(np.stack([np.asarray(inputs["c"], np.float32)[0], np.asarray(inputs["c_ctx"], np.float32)]))
    shared = {nm: np.ascontiguousarray(np.asarray(inputs[nm], dtype=np.float32)) for nm in WEIGHT_NAMES}
    big = {nm: (R, C) for nm, R, C in BIGW}
    maps = []
    for r in range(NCORE):
        m = {}
        for nm, a in shared.items():
            if nm in big:
                R, C = big[nm]
                a2 = a.reshape(DEPTH, R, C)
                m[nm] = np.ascontiguousarray(a2[:, r * (R // NCORE):(r + 1) * (R // NCORE), :])
            elif nm == "w_mod":
                m[nm] = np.ascontiguousarray(a[:, :, r * MODC:(r + 1) * MODC])
            elif nm == "b_mod":
                m[nm] = np.ascontiguousarray(a[:, r * MODC:(r + 1) * MODC])
            else:
                m[nm] = a
        m["x"] = np.ascontiguousarray(x[r * TL:(r + 1) * TL])
        m["ctx"] = ctx
        m["cc"] = cc
        m.update(_consts(r))
        maps.append(m)
    return maps


def kernel(**inputs):
    nc = Builder().build()
    maps = make_in_maps(inputs)
    res = run_bass_kernel_spmd(nc, maps, core_ids=list(range(NCORE)))
    out = np.concatenate([np.asarray(res.results[r]["out"], dtype=np.float32) for r in range(NCORE)], axis=0)
    return out[None]
```
